# Optimizing a Trainium2 kernel written in Bass

```python
import jax, jax.numpy as jnp
from jax import lax
import numpy as np

D_MODEL = 2048
BATCH = 4
SEQ = 4096
DEPTH = 1

HEAD_DIM = 128
N_Q_HEADS = 8
N_KV_HEADS = 2
GQA_GROUP = N_Q_HEADS // N_KV_HEADS
ATTN_WIDTH = N_Q_HEADS * HEAD_DIM
KV_WIDTH = N_KV_HEADS * HEAD_DIM
CONV_WIDTH = D_MODEL - ATTN_WIDTH
CONV_GROUPS = 8
MIX_WIDTH = ATTN_WIDTH + CONV_WIDTH
CONV_K = 3
GRID_W = 64
Q_BLOCK = 128
ROPE_THETA = 10000.0
ROPE_AXIS_DIM = HEAD_DIM // 2
EPS = 1e-6
SPLIT_SIZES = (ATTN_WIDTH, KV_WIDTH, KV_WIDTH, ATTN_WIDTH,
               CONV_WIDTH, CONV_WIDTH, CONV_WIDTH, CONV_WIDTH)
IN_PROJ_WIDTH = sum(SPLIT_SIZES)

kernel_name = "hymba_gqa_axialrope_shortconv_block"


def rmsnorm(x, g):
    xf = x.astype(jnp.float32)
    y = xf * lax.rsqrt(jnp.mean(xf * xf, axis=-1, keepdims=True) + EPS)
    return (y * g.astype(jnp.float32)).astype(x.dtype)


def axial_rope_tables(seq_len):
    rows = seq_len // GRID_W
    row = jnp.repeat(jnp.arange(rows, dtype=jnp.float32), GRID_W)
    col = jnp.tile(jnp.arange(GRID_W, dtype=jnp.float32), rows)
    inv_freq = ROPE_THETA ** (-jnp.arange(0, ROPE_AXIS_DIM, 2, dtype=jnp.float32) / ROPE_AXIS_DIM)
    ang_r = row[:, None] * inv_freq[None, :]
    ang_c = col[:, None] * inv_freq[None, :]
    return jnp.cos(ang_r), jnp.sin(ang_r), jnp.cos(ang_c), jnp.sin(ang_c)


def rotate(x, cos, sin):
    x1, x2 = jnp.split(x, 2, axis=-1)
    c = cos.astype(x.dtype)
    s = sin.astype(x.dtype)
    return jnp.concatenate([x1 * c - x2 * s, x2 * c + x1 * s], axis=-1)


def apply_axial_rope(x, tables):
    cr, sr, cc, sc = tables
    xr, xc = jnp.split(x, 2, axis=-1)
    return jnp.concatenate([rotate(xr, cr, sr), rotate(xc, cc, sc)], axis=-1)


def gqa_attention(q, k, v, q_norm, k_norm, tables):
    B, S, _ = q.shape
    q = rmsnorm(q.reshape(B, S, N_Q_HEADS, HEAD_DIM), q_norm).transpose(0, 2, 1, 3)
    k = rmsnorm(k.reshape(B, S, N_KV_HEADS, HEAD_DIM), k_norm).transpose(0, 2, 1, 3)
    v = v.reshape(B, S, N_KV_HEADS, HEAD_DIM).transpose(0, 2, 1, 3)
    q = apply_axial_rope(q, tables) * (HEAD_DIM ** -0.5)
    k = apply_axial_rope(k, tables)
    nb = S // Q_BLOCK
    qb = q.reshape(B, N_KV_HEADS, GQA_GROUP, nb, Q_BLOCK, HEAD_DIM).transpose(3, 0, 1, 2, 4, 5)

    def attend_block(qblk):
        s = jnp.einsum('bkgqd,bksd->bkgqs', qblk, k).astype(jnp.float32)
        p = jax.nn.softmax(s, axis=-1).astype(v.dtype)
        return jnp.einsum('bkgqs,bksd->bkgqd', p, v)

    o = lax.map(attend_block, qb)
    o = o.transpose(1, 0, 4, 2, 3, 5).reshape(B, S, ATTN_WIDTH)
    return o


def short_conv(xb, xc, xin, conv_w, conv_b):
    u = xc * xin
    S = u.shape[1]
    half = CONV_K // 2
    up = jnp.pad(u, ((0, 0), (half, half), (0, 0)))
    y = conv_b
    for j in range(CONV_K):
        y = y + up[:, j:j + S, :] * conv_w[j]
    return xb * y


def setup_inputs(seed: int = 0) -> dict:
    key = jax.random.key(seed)
    ks = jax.random.split(key, 10)
    f32 = jnp.float32
    x = jax.random.normal(ks[0], (BATCH, SEQ, D_MODEL), f32)
    norm_in = 1.0 + 0.02 * jax.random.normal(ks[1], (DEPTH, D_MODEL), f32)
    w_in = jax.random.normal(ks[2], (DEPTH, D_MODEL, IN_PROJ_WIDTH), f32) * D_MODEL ** -0.5
    q_norm = 1.0 + 0.02 * jax.random.normal(ks[3], (DEPTH, HEAD_DIM), f32)
    k_norm = 1.0 + 0.02 * jax.random.normal(ks[4], (DEPTH, HEAD_DIM), f32)
    conv_w = jax.random.normal(ks[5], (DEPTH, CONV_K, CONV_WIDTH), f32) * CONV_K ** -0.5
    conv_b = 0.02 * jax.random.normal(ks[6], (DEPTH, CONV_WIDTH), f32)
    w_out = jax.random.normal(ks[7], (DEPTH, MIX_WIDTH, D_MODEL), f32) * MIX_WIDTH ** -0.5
    norm_final = 1.0 + 0.02 * jax.random.normal(ks[8], (D_MODEL,), f32)
    return {"x": x, "norm_in": norm_in, "w_in": w_in, "q_norm": q_norm, "k_norm": k_norm,
            "conv_w": conv_w, "conv_b": conv_b, "w_out": w_out, "norm_final": norm_final}


def reference(x, norm_in, w_in, q_norm, k_norm, conv_w, conv_b, w_out, norm_final):
    S = x.shape[1]
    tables = axial_rope_tables(S)
    split_idx = list(np.cumsum(SPLIT_SIZES)[:-1])
    h = x
    for layer in range(DEPTH):
        hn = rmsnorm(h, norm_in[layer])
        proj = jnp.einsum('bsd,de->bse', hn, w_in[layer])
        q, k, v, g_attn, cb, cc, cx, g_conv = jnp.split(proj, split_idx, axis=-1)
        attn = gqa_attention(q, k, v, q_norm[layer], k_norm[layer], tables) * jax.nn.silu(g_attn)
        conv = short_conv(cb, cc, cx, conv_w[layer], conv_b[layer]) * jax.nn.silu(g_conv)
        mixed = jnp.concatenate([attn, conv], axis=-1)
        h = h + jnp.einsum('bse,ed->bsd', mixed, w_out[layer])
    return rmsnorm(h, norm_final)
```

```python
import contextlib
import numpy as np
import concourse.bass as bass
import concourse.mybir as mybir
from concourse.bass_utils import run_bass_kernel_spmd

F32 = mybir.dt.float32
BF16 = mybir.dt.bfloat16
AF = mybir.ActivationFunctionType
ALU = mybir.AluOpType
AX = mybir.AxisListType

D = 2048
S = 4096
NB = 4
NCORES = 8
HD = 128
NQH = 8
NKVH = 2
KC = 16
INW = 6656
OWN = 2048
PTOK = 512
NPASS = OWN // PTOK
NBLK = S // PTOK
EPS = 1e-6
OFF_Q, OFF_K, OFF_V, OFF_GA, OFF_CB, OFF_CC, OFF_CX, OFF_GC = 0, 1024, 1280, 1536, 2560, 3584, 4608, 5632
NGROUPS = INW // 128
SQ128 = float(np.sqrt(128.0))


class Buf:
    __slots__ = ("name", "last_w", "readers")

    def __init__(self, name):
        self.name = name
        self.last_w = None
        self.readers = []


class Chan:
    __slots__ = ("name", "count", "sem", "grouped")

    def __init__(self, name, grouped=False):
        self.name = name
        self.count = 0
        self.sem = None
        self.grouped = grouped


class Op:
    __slots__ = ("eng", "fn", "deps", "signal", "sigval", "chan", "dma_val", "waits")


ENGS = ["pe", "act", "dve", "pool", "sp"]


class Prog:
    def __init__(self, nc):
        self.nc = nc
        self.ops = {e: [] for e in ENGS}
        self.all = []
        self.chans = []

    def chan(self, name, grouped=False):
        c = Chan(name, grouped)
        self.chans.append(c)
        return c

    def add(self, eng, fn, r=(), w=(), chan=None):
        op = Op()
        op.eng = eng
        op.fn = fn
        op.signal = False
        op.sigval = 0
        op.chan = chan
        op.dma_val = 0
        deps = set()
        for b in r:
            if b.last_w is not None:
                deps.add(b.last_w)
        for b in w:
            if b.last_w is not None:
                deps.add(b.last_w)
            deps.update(b.readers)
        op.deps = [d for d in deps if not (d.eng == "pe" and eng == "pe" and d.chan is None)]
        for b in r:
            b.readers.append(op)
        for b in w:
            b.readers = []
            b.last_w = op
        if chan is not None:
            chan.count += 1
            op.dma_val = 16 * chan.count
        self.ops[eng].append(op)
        self.all.append(op)
        return op

    def finalize(self):
        for op in self.all:
            for d in op.deps:
                if d.chan is None:
                    d.signal = True
        for e in ENGS:
            c = 0
            for op in self.ops[e]:
                if op.signal:
                    c += 1
                    op.sigval = c
        for e in ENGS:
            waited = {}
            for op in self.ops[e]:
                need = {}
                for d in op.deps:
                    if d.chan is not None:
                        key = ("dma", d.chan)
                        val = 16 * d.chan.count if d.chan.grouped else d.dma_val
                    else:
                        key = ("eng", d.eng)
                        val = d.sigval
                    if need.get(key, 0) < val:
                        need[key] = val
                op.waits = []
                for key, val in need.items():
                    if waited.get(key, 0) >= val:
                        continue
                    waited[key] = val
                    op.waits.append((key, val))

    def emit(self, final_chans):
        nc = self.nc
        self.finalize()
        with contextlib.ExitStack() as es:
            engsem = {e: es.enter_context(nc.semaphore("es_" + e)) for e in ["pe", "act", "dve", "pool"]}
            for c in self.chans:
                c.sem = es.enter_context(nc.semaphore("ch_" + c.name))
            block = es.enter_context(nc.Block())

            def run(e, eng):
                for op in self.ops[e]:
                    for key, val in op.waits:
                        sem = engsem[key[1]] if key[0] == "eng" else key[1].sem
                        eng.wait_ge(sem, val)
                    ins = op.fn(eng)
                    if op.chan is not None:
                        ins.then_inc(op.chan.sem, 16)
                    elif op.signal:
                        ins.then_inc(engsem[e], 1)
                if e == "pool":
                    for c in final_chans:
                        eng.wait_ge(c.sem, 16 * c.count)

            @block.tensor
            def _(eng):
                run("pe", eng)

            @block.scalar
            def _(eng):
                run("act", eng)

            @block.vector
            def _(eng):
                run("dve", eng)

            @block.gpsimd
            def _(eng):
                run("pool", eng)

            @block.sync
            def _(eng):
                run("sp", eng)


def build_program():
    nc = bass.Bass("TRN2", target_bir_lowering=False)
    P = Prog(nc)

    def din(name, shape, dt=F32):
        return nc.dram_tensor(name, list(shape), dt, kind="ExternalInput").ap()

    x_all = din("x_all", [S, D])
    x_own = din("x_own", [OWN, D])
    x_halo = din("x_halo", [NPASS, 2, D])
    w_in = din("w_in", [D, INW])
    w_out = din("w_out", [D, D])
    g_in_d = din("g_in", [128, KC])
    gq_rep_d = din("gq_rep", [128, 128])
    gk_rep_d = din("gk_rep", [128, 128])
    convw_d = din("convw", [128, 8, 3])
    convb_d = din("convb", [128, 8])
    gfin_d = din("gfin", [128, D])
    tabk_d = din("tabk", [S, 2, 64])
    tabq_d = din("tabq", [OWN, 2, 64])
    cmat_d = din("cmat", [128, 2, 128])
    y = nc.dram_tensor("y", [OWN, D], F32, kind="ExternalOutput").ap()
    wscr = nc.dram_tensor("wscr", [NGROUPS, 128, KC * 128], BF16, kind="Internal").ap()

    def sb(name, shape, dt=F32):
        return nc.alloc_sbuf_tensor(name, list(shape), dt)

    NWB = 3
    NPT = 3
    KT = sb("KT", [128, NKVH, S], BF16)
    VV = sb("VV", [128, S // 128, NKVH * HD], BF16)
    WOUT = sb("WOUT", [128, KC, D], BF16)
    HNT = sb("HNT", [128, KC, PTOK + 2], BF16)
    MIXT = sb("MIXT", [128, KC, PTOK], BF16)
    STG = sb("STG", [128, KC, 128], F32)
    WBFALL = sb("WBFALL", [128, NWB * KC * 128], BF16)
    WBF = [WBFALL[:, i * KC * 128:(i + 1) * KC * 128].rearrange("p (a b) -> p a b", a=KC) for i in range(NWB)]
    STG2 = WBFALL[:, 0:2 * KC * 128].bitcast(F32).rearrange("p (a b) -> p a b", a=KC)
    QT = [sb("QT%d" % i, [128, PTOK], BF16) for i in range(2)]
    GATE = [sb("GATE%d" % i, [128, PTOK], F32) for i in range(2)]
    TABS = sb("TABS", [128, 4, 2, 64], F32)
    PTB = [sb("PTB%d" % i, [128, 2, PTOK], BF16) for i in range(NPT)]
    TMP = [sb("TMP%d" % i, [128, PTOK + 2], F32) for i in range(4)]
    QBT = sb("QBT", [128, 4, 128], BF16)
    JNK = sb("JNK", [128, 128], BF16)
    XS = [sb("XS%d" % i, [128, D], F32) for i in range(2)]
    HNBS = [sb("HNB%d" % i, [128, D], BF16) for i in range(2)]
    GFIN = sb("GFIN", [128, D], F32)
    G_IN = sb("G_IN", [128, KC], F32)
    GREP = sb("GREP", [128, 2, 128], F32)
    CONVW = sb("CONVW", [128, 8, 3], F32)
    CONVB = sb("CONVB", [128, 8], F32)
    CMAT32 = sb("CMAT32", [128, 2, 128], F32)
    CMAT = sb("CMAT", [128, 2, 128], BF16)
    SM = sb("SM", [128, 8], F32)
    NEGB = sb("NEGB", [128, 1], F32)
    MHALF = sb("MHALF", [128, 4], F32)
    SSC = sb("SSC", [128, 4], F32)
    RSC = sb("RSC", [128, 4], F32)
    SS4 = sb("SS4", [128, 4], F32)
    RS4 = sb("RS4", [128, 4], F32)
    PS = nc.alloc_psum_tensor("PS", [128, 8, 512], F32)

    IDENT = CMAT[:, 0, :]
    ONES32 = CMAT32[:, 1, :]
    ONES = CMAT[:, 1, :]

    bKT = [Buf("KT%d" % i) for i in range(NBLK)]
    bVV = [Buf("VV%d" % i) for i in range(NBLK)]
    bWOUT = [Buf("WOUT%d" % i) for i in range(KC)]
    bHNT = [Buf("HNT%d" % i) for i in range(5)]
    bMIXT = [Buf("MIXT%d" % i) for i in range(KC)]
    bSTG = Buf("STG")
    bWBF = [Buf("WBF%d" % i) for i in range(NWB)]
    bQT = [Buf("QT%d" % i) for i in range(2)]
    bGATE = [Buf("GATE%d" % i) for i in range(2)]
    bTABS = Buf("TABS")
    bPTB = [Buf("PTB%d" % i) for i in range(NPT)]
    bTMP = [Buf("TMP%d" % i) for i in range(4)]
    bQBT = Buf("QBT")
    bJNK = Buf("JNK")
    bXS = [Buf("XS%d" % i) for i in range(2)]
    bHNBS = [Buf("HNB%d" % i) for i in range(2)]
    bHNA1 = [Buf("HNA1_%d" % i) for i in range(4)]
    bSS = [Buf("SS%d" % i) for i in range(4)]
    bRS = [Buf("RS%d" % i) for i in range(4)]
    bSS4 = Buf("SS4")
    bRS4 = Buf("RS4")
    bPS = [Buf("PS%d" % i) for i in range(8)]
    bCONST = Buf("CONST")
    bSCR = [Buf("SCR%d" % i) for i in range(NGROUPS)]

    cCONST = P.chan("const", grouped=True)
    cSTG = P.chan("stg")
    cSTG2 = P.chan("stg2")
    cWBF = [P.chan("wbf%d" % i) for i in range(NWB)]
    cWBS = [P.chan("wbs%d" % i) for i in range(NWB)]
    cXS = [P.chan("xs%d" % i) for i in range(2)]
    cXST = [P.chan("xst%d" % i) for i in range(2)]
    cTABS = P.chan("tabs")

    const_ops = []

    def cload(dst, src):
        const_ops.append(P.add("sp", lambda e: e.dma_start(out=dst, in_=src), chan=cCONST))

    cload(G_IN[:, :], g_in_d)
    cload(GREP[:, 0, :], gq_rep_d)
    cload(GREP[:, 1, :], gk_rep_d)
    cload(CONVW[:, :, :], convw_d)
    cload(CONVB[:, :], convb_d)
    cload(CMAT32[:, :, :], cmat_d)
    cload(GFIN[:, :], gfin_d)
    bCONST.last_w = const_ops[-1]

    P.add("pool", lambda e: e.memset(MHALF[:, :], -0.5), r=[bCONST], w=[bCONST])
    P.add("dve", lambda e: e.tensor_copy(out=CMAT[:, :, :], in_=CMAT32[:, :, :]), r=[bCONST], w=[bCONST])
    P.add("dve", lambda e: e.reduce_max(out=SM[:, 0:1], in_=GREP[:, 0, :], axis=AX.X, apply_absolute_value=True),
          r=[bCONST], w=[bCONST])
    P.add("dve", lambda e: e.reduce_max(out=SM[:, 1:2], in_=GREP[:, 1, :], axis=AX.X, apply_absolute_value=True),
          r=[bCONST], w=[bCONST])
    P.add("dve", lambda e: e.tensor_tensor(out=SM[:, 2:3], in0=SM[:, 0:1], in1=SM[:, 1:2], op=ALU.mult),
          r=[bCONST], w=[bCONST])
    P.add("dve", lambda e: e.tensor_scalar(out=NEGB[:, :], in0=SM[:, 2:3], scalar1=-SQ128, scalar2=None, op0=ALU.mult),
          r=[bCONST], w=[bCONST])
    P.add("dve", lambda e: e.tensor_scalar(out=GREP[:, 0, :], in0=GREP[:, 0, :], scalar1=1.0 / SQ128, scalar2=None,
                                           op0=ALU.mult), r=[bCONST], w=[bCONST])

    state = {"wb": 0, "xs": 0, "ss": 0, "ps": 0, "qt": 0, "ptb": 0, "gen": 0, "sp": 0, "hnb": 0, "nrot": NWB - 1, "cvspace": 3, "cvspace2": 2, "stgalt": 0}
    scr_ready = [False] * NGROUPS

    def next_ps():
        i = state["ps"]
        state["ps"] = (i + 1) % 8
        return i

    def next_ps_pair():
        i = state["ps"]
        if i % 2:
            i = (i + 1) % 8
        state["ps"] = (i + 2) % 8
        return i

    def next_gen(exclude=()):
        while True:
            i = 4 + state["gen"]
            state["gen"] = (state["gen"] + 1) % 4
            if i not in exclude:
                return i

    def next_spair():
        i = state["sp"]
        state["sp"] = (i + 2) % 4
        return i

    def load_group(col0):
        k = state["wb"] % state["nrot"]
        state["wb"] = (k + 1) % state["nrot"]
        wb, bwb, cwb, cws = WBF[k], bWBF[k], cWBF[k], cWBS[k]
        gidx = col0 // 128
        if scr_ready[gidx]:
            P.add("sp", lambda e: e.dma_start(out=wb[:, :, :].rearrange("p a b -> p (a b)"), in_=wscr[gidx]),
                  r=[bSCR[gidx]], w=[bwb], chan=cwb)
            return wb, bwb
        assert not state.get("strict"), "weight group %d used before its conversion was emitted" % gidx
        srcap = w_in[:, col0:col0 + 128].rearrange("(kc p) c -> p kc c", p=128)
        P.add("sp", lambda e: e.dma_start(out=STG[:, :, :], in_=srcap), w=[bSTG], chan=cSTG)
        gb = G_IN[:, :].unsqueeze(2).to_broadcast([128, KC, 128])
        P.add("dve", lambda e: e.tensor_tensor(out=wb[:, :, :], in0=STG[:, :, :], in1=gb, op=ALU.mult),
              r=[bSTG, bCONST], w=[bwb])
        P.add("pool", lambda e: e.dma_start(out=wscr[gidx], in_=wb[:, :, :].rearrange("p a b -> p (a b)")),
              r=[bwb], w=[bSCR[gidx]], chan=cws)
        scr_ready[gidx] = True
        return wb, bwb

    XENG = "sp"

    def x_issue(src_rows, npart):
        xi = state["xs"]
        state["xs"] = 1 - xi
        xs_ = XS[xi]
        P.add(XENG, lambda e: e.dma_start(out=xs_[0:npart, :], in_=src_rows), w=[bXS[xi]], chan=cXS[xi])
        return xi

    def front_tiles(tiles, alloc_pair, HN=None):
        nxt = x_issue(tiles[0][0], tiles[0][1])
        for i, (src_rows, npart, tok0, hb) in enumerate(tiles):
            xi = nxt
            if i + 1 < len(tiles):
                nxt = x_issue(tiles[i + 1][0], tiles[i + 1][1])
            yield from front_tile(xi, npart, tok0, hb, alloc_pair, HN=HN)

    def front_tile(xi, npart, tok0, hb, alloc_pair, HN=None):
        si = state["ss"]
        state["ss"] = (si + 1) % 4
        xs, bxs = XS[xi], bXS[xi]
        if HN is None:
            HN = HNT
        hi = state["hnb"]
        state["hnb"] = 1 - hi
        HNB, bHNB = HNBS[hi], bHNBS[hi]
        P.add("dve", lambda e: e.memset(SSC[0:npart, si:si + 1], 0.0), w=[bSS[si]])
        P.add("act", lambda e: e.activation(out=HNB[0:npart, :], in_=xs[0:npart, :], func=AF.Square,
                                            accum_out=SSC[0:npart, si:si + 1]),
              r=[bxs], w=[bHNB, bSS[si]])
        yield
        P.add("dve", lambda e: e.tensor_scalar(out=RSC[0:npart, si:si + 1], in0=SSC[0:npart, si:si + 1],
                                               scalar1=1.0 / D, scalar2=EPS, op0=ALU.mult, op1=ALU.add),
              r=[bSS[si]], w=[bRS[si]])
        P.add("pool", lambda e: e.tensor_tensor(out=RSC[0:npart, si:si + 1], in0=RSC[0:npart, si:si + 1],
                                                in1=MHALF[0:npart, 0:1], op=ALU.pow),
              r=[bRS[si], bCONST], w=[bRS[si]])
        yield
        P.add("dve", lambda e: e.tensor_scalar(out=HNB[0:npart, :], in0=xs[0:npart, :],
                                               scalar1=RSC[0:npart, si:si + 1], scalar2=None, op0=ALU.mult),
              r=[bxs, bRS[si]], w=[bHNB])
        yield
        pb = alloc_pair()
        psb = PS[:, pb:pb + 2, :].bitcast(BF16)
        for kc in range(KC):
            o = psb[:, kc // 8, (kc % 8) * 128:(kc % 8) * 128 + npart]

            def tr(e, o=o, kc=kc):
                return e.transpose(out=o, in_=HNB[0:npart, kc * 128:(kc + 1) * 128], identity=IDENT[0:npart, 0:npart])
            P.add("pe", tr, r=[bHNB, bCONST], w=[bPS[pb + kc // 8]])
        yield
        src0 = psb[:, 0, :].rearrange("p (a b) -> p a b", a=8)[:, :, 0:npart]
        src1 = psb[:, 1, :].rearrange("p (a b) -> p a b", a=8)[:, :, 0:npart]
        P.add("act", lambda e: e.copy(out=HN[:, 0:8, tok0:tok0 + npart], in_=src0), r=[bPS[pb]], w=[hb])
        P.add("dve", lambda e: e.tensor_copy(out=HN[:, 8:16, tok0:tok0 + npart], in_=src1), r=[bPS[pb + 1]], w=[hb])
        yield

    def proj_fm(wb, bwb, bank, ntok=PTOK, tok0=0, col0=0, ychunk=4):
        hbs = bHNT[0:4] if tok0 == 0 else [bHNT[4]]
        for kc in range(KC):
            def mm(e, kc=kc):
                return e.matmul(out=PS[:, bank, col0:col0 + ntok], lhsT=wb[:, kc, :], rhs=HNT[:, kc, tok0:tok0 + ntok],
                                start=(kc == 0), stop=(kc == KC - 1))
            P.add("pe", mm, r=[bwb] + hbs, w=[bPS[bank]])
            if kc % ychunk == ychunk - 1:
                yield

    def proj_tm(wb, bwb, bank, ychunk=8, HN=None, hbufs=None):
        if HN is None:
            HN, hbufs = HNT, bHNT
        for tt in range(4):
            for kc in range(KC):
                def mm(e, kc=kc, tt=tt):
                    return e.matmul(out=PS[:, bank, tt * 128:(tt + 1) * 128], lhsT=HN[:, kc, tt * 128:(tt + 1) * 128],
                                    rhs=wb[:, kc, :], start=(kc == 0), stop=(kc == KC - 1))
                P.add("pe", mm, r=[bwb, hbufs[tt]], w=[bPS[bank]])
                if kc % ychunk == ychunk - 1:
                    yield

    def qk_post(bank, gi, out_T, out_bufs, tbank, nspace=0):
        QP = PS[:, bank, :].rearrange("p (t d) -> p t d", t=4)
        P.add("dve", lambda e: e.memset(SS4[:, :], 0.0), w=[bSS4])
        for tt in range(4):
            P.add("act", lambda e, tt=tt: e.activation(out=JNK[:, :], in_=QP[:, tt, :], func=AF.Square,
                                                       accum_out=SS4[:, tt:tt + 1]),
                  r=[bPS[bank]], w=[bJNK, bSS4])
        yield
        P.add("dve", lambda e: e.tensor_scalar(out=RS4[:, :], in0=SS4[:, :], scalar1=1.0 / HD, scalar2=EPS,
                                               op0=ALU.mult, op1=ALU.add), r=[bSS4], w=[bRS4])
        P.add("pool", lambda e: e.tensor_tensor(out=RS4[:, :], in0=RS4[:, :], in1=MHALF[:, :], op=ALU.pow),
              r=[bRS4, bCONST], w=[bRS4])
        yield
        QN = TMP[0][:, 0:PTOK].rearrange("p (t d) -> p t d", t=4)
        rsb = RS4[:, :].unsqueeze(2).to_broadcast([128, 4, 128])
        gbr = GREP[:, gi, :].unsqueeze(1).to_broadcast([128, 4, 128])
        P.add("dve", lambda e: e.tensor_tensor(out=QN, in0=QP, in1=rsb, op=ALU.mult), r=[bPS[bank], bRS4], w=[bTMP[0]])
        P.add("dve", lambda e: e.tensor_tensor(out=QN, in0=QN, in1=gbr, op=ALU.mult), r=[bTMP[0], bCONST], w=[bTMP[0]])
        yield
        X = TMP[0][:, 0:PTOK].rearrange("p (t s h i) -> p t s h i", t=4, s=2, h=2)
        x1, x2 = X[:, :, :, 0, :], X[:, :, :, 1, :]
        Ct = TABS[:, :, 0, :].rearrange("p t (s i) -> p t s i", s=2)
        St = TABS[:, :, 1, :].rearrange("p t (s i) -> p t s i", s=2)
        A = TMP[1][:, 0:256].rearrange("p (t s i) -> p t s i", t=4, s=2)
        B = TMP[2][:, 0:256].rearrange("p (t s i) -> p t s i", t=4, s=2)
        QB = QBT[:, :, :].rearrange("p t (s h i) -> p t s h i", s=2, h=2)
        P.add("dve", lambda e: e.tensor_tensor(out=A, in0=x1, in1=Ct, op=ALU.mult), r=[bTMP[0], bTABS], w=[bTMP[1]])
        P.add("dve", lambda e: e.tensor_tensor(out=B, in0=x2, in1=St, op=ALU.mult), r=[bTMP[0], bTABS], w=[bTMP[2]])
        P.add("dve", lambda e: e.tensor_tensor(out=QB[:, :, :, 0, :], in0=A, in1=B, op=ALU.subtract),
              r=[bTMP[1], bTMP[2]], w=[bQBT])
        yield
        P.add("dve", lambda e: e.tensor_tensor(out=A, in0=x2, in1=Ct, op=ALU.mult), r=[bTMP[0], bTABS], w=[bTMP[1]])
        P.add("dve", lambda e: e.tensor_tensor(out=B, in0=x1, in1=St, op=ALU.mult), r=[bTMP[0], bTABS], w=[bTMP[2]])
        P.add("dve", lambda e: e.tensor_tensor(out=QB[:, :, :, 1, :], in0=A, in1=B, op=ALU.add),
              r=[bTMP[1], bTMP[2]], w=[bQBT])
        yield
        for _ in range(nspace):
            yield
        psb = PS[:, tbank, :].bitcast(BF16)
        for tt in range(4):
            P.add("pe", lambda e, tt=tt: e.transpose(out=psb[:, tt * 128:(tt + 1) * 128], in_=QBT[:, tt, :], identity=IDENT),
                  r=[bQBT, bCONST], w=[bPS[tbank]])
        yield
        P.add("act", lambda e: e.copy(out=out_T, in_=psb[:, 0:PTOK]), r=[bPS[tbank]], w=out_bufs)
        yield

    def load_tabs(src_rows):
        src = src_rows.rearrange("(t p) c f -> p t c f", p=128)
        P.add("sp", lambda e: e.dma_start(out=TABS[:, :, :, :], in_=src), w=[bTABS], chan=cTABS)

    def drain(g):
        for _ in g:
            pass

    def gen_front(ps_i, alloc_pair):
        t0 = ps_i * PTOK
        tiles = [(x_own[t0 + tt * 128:t0 + (tt + 1) * 128, :], 128, tt * 128, bHNT[tt]) for tt in range(4)]
        tiles.append((x_halo[ps_i, :, :], 2, PTOK, bHNT[4]))
        yield from front_tiles(tiles, alloc_pair)

    def load_wout(ec):
        src = w_out[ec * 128:(ec + 1) * 128, :]
        P.add("sp", lambda e: e.dma_start(out=STG[:, :, :].rearrange("p a b -> p (a b)"), in_=src), w=[bSTG], chan=cSTG)
        P.add("dve", lambda e: e.tensor_copy(out=WOUT[:, ec, :], in_=STG[:, :, :].rearrange("p a b -> p (a b)")),
              r=[bSTG], w=[bWOUT[ec]])

    cCVS = P.chan("cvs")

    def gen_convert(col0):
        stg, bstg, cstg = STG, [bSTG], cSTG
        if state["nrot"] < NWB:
            k = NWB - 1
            wb, bwb, cws = WBF[k], bWBF[k], cWBS[k]
            if state["stgalt"] % 2 == 1:
                stg, bstg, cstg = STG2, [bWBF[0], bWBF[1]], cSTG2
            state["stgalt"] += 1
        else:
            wb, bwb, cws = HNBS[0][:, :].rearrange("p (a b) -> p a b", a=KC), bHNBS[0], cCVS
        gidx = col0 // 128
        srcap = w_in[:, col0:col0 + 128].rearrange("(kc p) c -> p kc c", p=128)
        P.add("sp", lambda e: e.dma_start(out=stg[:, :, :], in_=srcap), w=bstg, chan=cstg)
        yield
        for _ in range(state["cvspace"]):
            yield
        gb = G_IN[:, :].unsqueeze(2).to_broadcast([128, KC, 128])
        P.add("dve", lambda e: e.tensor_tensor(out=wb[:, :, :], in0=stg[:, :, :], in1=gb, op=ALU.mult),
              r=bstg + [bCONST], w=[bwb])
        P.add("pool", lambda e: e.dma_start(out=wscr[gidx], in_=wb[:, :, :].rearrange("p a b -> p (a b)")),
              r=[bwb], w=[bSCR[gidx]], chan=cws)
        scr_ready[gidx] = True
        yield
        for _ in range(state["cvspace2"]):
            yield

    def gen_wout(ec):
        src = w_out[ec * 128:(ec + 1) * 128, :]
        P.add("sp", lambda e: e.dma_start(out=STG[:, :, :].rearrange("p a b -> p (a b)"), in_=src), w=[bSTG], chan=cSTG)
        yield
        yield
        yield
        P.add("dve", lambda e: e.tensor_copy(out=WOUT[:, ec, :], in_=STG[:, :, :].rearrange("p a b -> p (a b)")),
              r=[bSTG], w=[bWOUT[ec]])
        yield

    def gen_preconv():
        order = []
        for j in range(8):
            order += [OFF_CC + j * 128, OFF_CX + j * 128, OFF_GC + j * 128, OFF_CB + j * 128]
        for h in range(NQH):
            order += [OFF_Q + h * 128, OFF_GA + h * 128]
        for n, col0 in enumerate(order[:48]):
            yield from gen_convert(col0)
            if n >= 40 and n % 2 == 1:
                for ec in range(n - 41, n - 39):
                    yield from gen_wout(ec)
        for ec in range(8, KC):
            yield from gen_wout(ec)

    preconv = gen_preconv()

    def interleave(main, fill, ratio):
        n = 0
        for _ in main:
            n += 1
            if n % ratio == 0 and fill is not None:
                next(fill, None)

    HNA = [(HNT, bHNT), (MIXT, bHNA1)]

    rrs = {"kv": 0, "fr": 0}

    def kv_bank():
        i = rrs["kv"]
        rrs["kv"] = (i + 1) % 4
        return i

    def fr_pair():
        i = rrs["fr"]
        rrs["fr"] = 1 - i
        return 4 + 2 * i

    def gen_front_A(tb):
        HN, hb = HNA[tb % 2]
        tiles = [(x_all[tb * PTOK + tt * 128:tb * PTOK + (tt + 1) * 128, :], 128, tt * 128, hb[tt]) for tt in range(4)]
        yield from front_tiles(tiles, fr_pair, HN=HN)

    kv_cols = [OFF_K, OFF_K + 128, OFF_V, OFF_V + 128]
    WKV = [WOUT[:, :, g * 128:(g + 1) * 128] for g in range(4)]
    bWKV = [Buf("WKV%d" % g) for g in range(4)]
    kvq = {"n": 0}

    def kv_weights():
        g = kvq["n"] % 4
        kvq["n"] += 1
        return WKV[g], bWKV[g]

    def gen_kv_convert(g):
        srcap = w_in[:, kv_cols[g]:kv_cols[g] + 128].rearrange("(kc p) c -> p kc c", p=128)
        P.add("sp", lambda e: e.dma_start(out=STG[:, :, :], in_=srcap), w=[bSTG], chan=cSTG)
        yield
        yield
        gb = G_IN[:, :].unsqueeze(2).to_broadcast([128, KC, 128])
        P.add("dve", lambda e: e.tensor_tensor(out=WKV[g], in0=STG[:, :, :], in1=gb, op=ALU.mult),
              r=[bSTG, bCONST], w=[bWKV[g]])
        yield

    def gen_kv(tb):
        HN, hb = HNA[tb % 2]
        load_tabs(tabk_d[tb * PTOK:(tb + 1) * PTOK])
        for kvh in range(NKVH):
            wb, bwb = kv_weights()
            bank = kv_bank()
            yield from proj_tm(wb, bwb, bank, HN=HN, hbufs=hb)
            yield from qk_post(bank, 1, KT[:, kvh, tb * PTOK:(tb + 1) * PTOK], [bKT[tb]], kv_bank(), nspace=3)
        for kvh in range(NKVH):
            wb, bwb = kv_weights()
            bank = kv_bank()
            yield from proj_tm(wb, bwb, bank, HN=HN, hbufs=hb)

            def evac(e, bank=bank, kvh=kvh, tb=tb):
                return e.copy(out=VV[:, tb * 4:(tb + 1) * 4, kvh * HD:(kvh + 1) * HD],
                              in_=PS[:, bank, :].rearrange("p (a b) -> p a b", a=4))
            P.add("act", evac, r=[bPS[bank]], w=[bVV[tb]])
            yield

    def chain(*gens):
        for g in gens:
            if g is not None:
                yield from g

    def take(g, n):
        for _ in range(n):
            if next(g, "END") == "END":
                return
            yield

    def rr(streams):
        live = [[g, w] for g, w in streams if g is not None]
        while live:
            for item in list(live):
                g, w = item
                for _ in range(w):
                    if next(g, "END") == "END":
                        live.remove(item)
                        break

    rr([(gen_front_A(0), 1),
        (chain(*[gen_kv_convert(g) for g in range(4)]), 1)])
    for tb in range(NBLK):
        nxt = gen_front_A(tb + 1) if tb + 1 < NBLK else gen_front(0, fr_pair)
        rr([(gen_kv(tb), 2), (nxt, 1), (take(preconv, 28), 1)])
    alias_ops = []
    for b in bHNA1:
        alias_ops += list(b.readers)
        if b.last_w is not None:
            alias_ops.append(b.last_w)
    for b in bMIXT:
        b.readers = list(alias_ops)
    alias_ops = []
    for b in bWKV:
        alias_ops += list(b.readers)
        if b.last_w is not None:
            alias_ops.append(b.last_w)
    for b in bWOUT:
        b.readers = list(alias_ops)

    wh = {}

    def issue(h, which):
        if h < NQH:
            wh[(h, which)] = load_group((OFF_Q if which == "q" else OFF_GA) + h * 128)

    def gen_head_inproj(ps_i, h, qi, alloc):
        wb, bwb = wh.pop((h, "q"))
        bq = alloc()
        yield from proj_tm(wb, bwb, bq)
        yield from qk_post(bq, 0, QT[qi][:, :], [bQT[qi]], alloc(), nspace=8)
        wb, bwb = wh.pop((h, "ga"))
        bga = alloc()
        yield from proj_fm(wb, bwb, bga)
        G = GATE[qi]
        P.add("act", lambda e: e.activation(out=G[:, :], in_=PS[:, bga, :], func=AF.Tanh, scale=0.5),
              r=[bPS[bga]], w=[bGATE[qi]])
        P.add("dve", lambda e: e.scalar_tensor_tensor(out=G[:, :], in0=G[:, :], scalar=1.0, in1=PS[:, bga, :],
                                                      op0=ALU.add, op1=ALU.mult), r=[bPS[bga], bGATE[qi]], w=[bGATE[qi]])
        yield

    SUMENG = "pool"

    def qk_emit(kvh, qi, k2):
        sp_ = next_spair()
        for i in range(2):
            kt = 2 * k2 + i

            def mm(e, kt=kt, i=i, sp_=sp_):
                return e.matmul(out=PS[:, sp_ + i, :], lhsT=KT[:, kvh, kt * 128:(kt + 1) * 128], rhs=QT[qi][:, :],
                                start=True, stop=True)
            P.add("pe", mm, r=[bKT[kt // 4], bQT[qi]], w=[bPS[sp_ + i]])
        return sp_

    def attention(h, qi, ob, lb, filler, nfill, bg=None, sp0=None, nxt=None, deferred=None):
        kvh = h // (NQH // NKVH)
        nkt = S // 128
        nst = nkt // 2
        sps = {0: sp0 if sp0 is not None else qk_emit(kvh, qi, 0)}
        sp_next = None
        for k2 in range(nst):
            if k2 + 1 < nst:
                sps[k2 + 1] = qk_emit(kvh, qi, k2 + 1)
            elif nxt is not None:
                sp_next = qk_emit(nxt[1], nxt[0], 0)
            sp_ = sps[k2]
            pi = state["ptb"]
            state["ptb"] = (pi + 1) % NPT

            def ex(e, sp_=sp_, pi=pi):
                return e.activation(out=PTB[pi][:, :, :], in_=PS[:, sp_:sp_ + 2, :], func=AF.Exp,
                                    bias=NEGB[:, 0:1], scale=1.0)
            P.add("act", ex, r=[bPS[sp_], bPS[sp_ + 1], bCONST], w=[bPTB[pi]])
            for i in range(2):
                kt = 2 * k2 + i

                def pv(e, kt=kt, i=i, pi=pi):
                    return e.matmul(out=PS[:, ob, :], lhsT=VV[:, kt, kvh * HD:(kvh + 1) * HD], rhs=PTB[pi][:, i, :],
                                    start=(kt == 0), stop=(kt == nkt - 1))
                P.add("pe", pv, r=[bVV[kt // 4], bPTB[pi]], w=[bPS[ob]])
            for i in range(2):
                kt = 2 * k2 + i

                def ls(e, kt=kt, i=i, pi=pi):
                    return e.matmul(out=PS[:, lb, :], lhsT=ONES, rhs=PTB[pi][:, i, :],
                                    start=(kt == 0), stop=(kt == nkt - 1))
                P.add("pe", ls, r=[bCONST, bPTB[pi]], w=[bPS[lb]])
            if k2 == 0 and deferred is not None:
                deferred()
            if filler is not None:
                for _ in range(nfill):
                    next(filler, None)
                if k2 == nst - 3 and nxt is not None:
                    drain(filler)
            if k2 == 3:
                issue(h + 2, "q")
            if k2 == 10:
                issue(h + 2, "ga")
            if bg is not None:
                next(bg, None)
        def fin():
            RL = TMP[3][:, 0:PTOK]
            O = RL
            P.add("act", lambda e: e.copy(out=RL, in_=PS[:, lb, :]), r=[bPS[lb]], w=[bTMP[3]])
            P.add("dve", lambda e: e.reciprocal(out=RL, in_=RL), r=[bTMP[3]], w=[bTMP[3]])
            P.add("dve", lambda e: e.tensor_tensor(out=O, in0=PS[:, ob, :], in1=RL, op=ALU.mult),
                  r=[bPS[ob], bTMP[3]], w=[bTMP[3]])
            P.add("dve", lambda e: e.scalar_tensor_tensor(out=MIXT[:, h, :], in0=O, scalar=0.5, in1=GATE[qi][:, :],
                                                          op0=ALU.mult, op1=ALU.mult),
                  r=[bTMP[3], bGATE[qi]], w=[bMIXT[h]])
        return fin, sp_next

    def conv_chunk(j):
        hbank = next_ps()
        CCE, U, Y, SG = TMP[0], TMP[1], TMP[2], TMP[3]
        wb, bwb = load_group(OFF_CC + j * 128)
        bcc = next_ps()
        drain(proj_fm(wb, bwb, bcc))
        drain(proj_fm(wb, bwb, hbank, ntok=2, tok0=PTOK, col0=0))
        P.add("act", lambda e: e.copy(out=CCE[:, 1:PTOK + 1], in_=PS[:, bcc, :]), r=[bPS[bcc]], w=[bTMP[0]])
        P.add("act", lambda e: e.copy(out=CCE[:, 0:1], in_=PS[:, hbank, 0:1]), r=[bPS[hbank]], w=[bTMP[0]])
        P.add("act", lambda e: e.copy(out=CCE[:, PTOK + 1:PTOK + 2], in_=PS[:, hbank, 1:2]), r=[bPS[hbank]], w=[bTMP[0]])
        wb, bwb = load_group(OFF_CX + j * 128)
        bcx = next_ps()
        drain(proj_fm(wb, bwb, bcx))
        drain(proj_fm(wb, bwb, hbank, ntok=2, tok0=PTOK, col0=8))
        P.add("dve", lambda e: e.tensor_tensor(out=U[:, 1:PTOK + 1], in0=PS[:, bcx, :], in1=CCE[:, 1:PTOK + 1], op=ALU.mult),
              r=[bPS[bcx], bTMP[0]], w=[bTMP[1]])
        P.add("dve", lambda e: e.tensor_tensor(out=U[:, 0:1], in0=PS[:, hbank, 8:9], in1=CCE[:, 0:1], op=ALU.mult),
              r=[bPS[hbank], bTMP[0]], w=[bTMP[1]])
        P.add("dve", lambda e: e.tensor_tensor(out=U[:, PTOK + 1:PTOK + 2], in0=PS[:, hbank, 9:10],
                                               in1=CCE[:, PTOK + 1:PTOK + 2], op=ALU.mult),
              r=[bPS[hbank], bTMP[0]], w=[bTMP[1]])
        P.add("dve", lambda e: e.tensor_scalar(out=Y[:, 0:PTOK], in0=U[:, 0:PTOK], scalar1=CONVW[:, j, 0:1],
                                               scalar2=CONVB[:, j:j + 1], op0=ALU.mult, op1=ALU.add),
              r=[bTMP[1], bCONST], w=[bTMP[2]])
        P.add("dve", lambda e: e.scalar_tensor_tensor(out=Y[:, 0:PTOK], in0=U[:, 1:PTOK + 1], scalar=CONVW[:, j, 1:2],
                                                      in1=Y[:, 0:PTOK], op0=ALU.mult, op1=ALU.add),
              r=[bTMP[1], bTMP[2], bCONST], w=[bTMP[2]])
        P.add("dve", lambda e: e.scalar_tensor_tensor(out=Y[:, 0:PTOK], in0=U[:, 2:PTOK + 2], scalar=CONVW[:, j, 2:3],
                                                      in1=Y[:, 0:PTOK], op0=ALU.mult, op1=ALU.add),
              r=[bTMP[1], bTMP[2], bCONST], w=[bTMP[2]])
        wb, bwb = load_group(OFF_GC + j * 128)
        bgc = next_ps()
        drain(proj_fm(wb, bwb, bgc))
        P.add("act", lambda e: e.activation(out=SG[:, 0:PTOK], in_=PS[:, bgc, :], func=AF.Tanh, scale=0.5),
              r=[bPS[bgc]], w=[bTMP[3]])
        P.add("dve", lambda e: e.scalar_tensor_tensor(out=SG[:, 0:PTOK], in0=SG[:, 0:PTOK], scalar=1.0, in1=PS[:, bgc, :],
                                                      op0=ALU.add, op1=ALU.mult), r=[bPS[bgc], bTMP[3]], w=[bTMP[3]])
        wb, bwb = load_group(OFF_CB + j * 128)
        bcb = next_ps()
        drain(proj_fm(wb, bwb, bcb))
        P.add("dve", lambda e: e.tensor_tensor(out=Y[:, 0:PTOK], in0=PS[:, bcb, :], in1=Y[:, 0:PTOK], op=ALU.mult),
              r=[bPS[bcb], bTMP[2]], w=[bTMP[2]])
        P.add("dve", lambda e: e.scalar_tensor_tensor(out=MIXT[:, 8 + j, :], in0=Y[:, 0:PTOK], scalar=0.5,
                                                      in1=SG[:, 0:PTOK], op0=ALU.mult, op1=ALU.mult),
              r=[bTMP[2], bTMP[3]], w=[bMIXT[8 + j]])

    def out_proj(ps_i):
        t0 = ps_i * PTOK
        for tt in range(4):
            r0 = t0 + tt * 128
            xi = state["xs"]
            state["xs"] = 1 - xi
            si = state["ss"]
            state["ss"] = (si + 1) % 4
            xs, bxs = XS[xi], bXS[xi]
            hi = state["hnb"]
            state["hnb"] = 1 - hi
            HNB, bHNB = HNBS[hi], bHNBS[hi]
            P.add(XENG, lambda e, xs=xs, r0=r0: e.dma_start(out=xs[:, :], in_=x_own[r0:r0 + 128, :]), w=[bxs], chan=cXS[xi])
            pb = 0 if state["ps"] < 4 else 4
            state["ps"] = (pb + 4) % 8
            for dmb in range(4):
                for ec in range(KC):
                    def mm(e, ec=ec, dmb=dmb, pb=pb, tt=tt):
                        return e.matmul(out=PS[:, pb + dmb, :], lhsT=MIXT[:, ec, tt * 128:(tt + 1) * 128],
                                        rhs=WOUT[:, ec, dmb * 512:(dmb + 1) * 512], start=(ec == 0), stop=(ec == KC - 1))
                    P.add("pe", mm, r=[bMIXT[ec], bWOUT[ec]], w=[bPS[pb + dmb]])
            pbufs = [bPS[pb + i] for i in range(4)]
            P.add("dve", lambda e, xs=xs, pb=pb: e.tensor_tensor(out=xs[:, :], in0=PS[:, pb:pb + 4, :].rearrange("p a b -> p (a b)"),
                                                                 in1=xs[:, :], op=ALU.add), r=pbufs + [bxs], w=[bxs])
            P.add("dve", lambda e, si=si: e.memset(SSC[:, si:si + 1], 0.0), w=[bSS[si]])
            P.add("act", lambda e, xs=xs, si=si, HNB=HNB: e.activation(out=HNB[:, :], in_=xs[:, :], func=AF.Square,
                                                                       accum_out=SSC[:, si:si + 1]), r=[bxs], w=[bHNB, bSS[si]])
            P.add("dve", lambda e, si=si: e.tensor_scalar(out=RSC[:, si:si + 1], in0=SSC[:, si:si + 1], scalar1=1.0 / D,
                                                          scalar2=EPS, op0=ALU.mult, op1=ALU.add), r=[bSS[si]], w=[bRS[si]])
            P.add("pool", lambda e, si=si: e.tensor_tensor(out=RSC[:, si:si + 1], in0=RSC[:, si:si + 1],
                                                           in1=MHALF[:, 0:1], op=ALU.pow), r=[bRS[si], bCONST], w=[bRS[si]])
            P.add("dve", lambda e, xs=xs, si=si: e.scalar_tensor_tensor(out=xs[:, :], in0=xs[:, :], scalar=RSC[:, si:si + 1],
                                                                        in1=GFIN[:, :], op0=ALU.mult, op1=ALU.mult),
                  r=[bxs, bRS[si], bCONST], w=[bxs])
            P.add("pool", lambda e, xs=xs, r0=r0: e.dma_start(out=y[r0:r0 + 128, :], in_=xs[:, :]), r=[bxs], chan=cXST[xi])

    state["nrot"] = NWB
    state["strict"] = True
    state["cvspace"] = 2
    state["cvspace2"] = 0
    for ps_i in range(NPASS):
        t0 = ps_i * PTOK
        load_tabs(tabq_d[t0:t0 + PTOK])
        for j in range(8):
            conv_chunk(j)
            if ps_i == 0:
                drain(take(preconv, 4))
        state["gen"] = 0
        state["sp"] = 0
        qis = [h % 2 for h in range(NQH)]
        issue(0, "q")
        issue(0, "ga")
        issue(1, "q")
        drain(gen_head_inproj(ps_i, 0, qis[0], lambda: next_gen()))
        issue(1, "ga")
        deferred, sp0 = None, None
        for h in range(NQH):
            ob = 4 + 2 * (h % 2)
            lb = ob + 1
            nxt = (qis[h + 1], (h + 1) // (NQH // NKVH)) if h + 1 < NQH else None
            if h + 1 < NQH:
                pob = 4 + 2 * ((h + 1) % 2)
                seq = iter([pob + 1, pob, pob + 1, pob, pob + 1, pob])
                filler = gen_head_inproj(ps_i, h + 1, qis[h + 1], lambda seq=seq: next(seq))
                nfill = 3
            elif ps_i + 1 < NPASS:
                filler = gen_front(ps_i + 1, lambda ob=ob: 4 if ob == 6 else 6)
                nfill = 2
            else:
                filler, nfill = None, 0
            deferred, sp0 = attention(h, qis[h], ob, lb, filler, nfill, bg=(preconv if ps_i == 0 else None),
                                      sp0=sp0, nxt=nxt, deferred=deferred)
            if filler is not None:
                drain(filler)
        deferred()
        if ps_i == 0:
            drain(preconv)
        out_proj(ps_i)

    P.emit(final_chans=cXST + cWBS)
    return nc


def _rope_tables():
    grid_w = 64
    rows = S // grid_w
    row = np.repeat(np.arange(rows, dtype=np.float32), grid_w)
    col = np.tile(np.arange(grid_w, dtype=np.float32), rows)
    inv_freq = (np.float32(10000.0) ** (-(np.arange(0, 64, 2, dtype=np.float32)) / np.float32(64.0))).astype(np.float32)
    ang_r = (row[:, None] * inv_freq[None, :]).astype(np.float32)
    ang_c = (col[:, None] * inv_freq[None, :]).astype(np.float32)
    tab = np.zeros((S, 2, 64), dtype=np.float32)
    tab[:, 0, 0:32] = np.cos(ang_r)
    tab[:, 0, 32:64] = np.cos(ang_c)
    tab[:, 1, 0:32] = np.sin(ang_r)
    tab[:, 1, 32:64] = np.sin(ang_c)
    return tab


def _const_mats():
    cm = np.zeros((128, 2, 128), dtype=np.float32)
    cm[:, 0, :] = np.eye(128, dtype=np.float32)
    cm[:, 1, :] = 1.0
    return cm


_NC_CACHE = {}


def kernel(x, norm_in, w_in, q_norm, k_norm, conv_w, conv_b, w_out, norm_final):
    x = np.asarray(x, dtype=np.float32)
    w_in0 = np.ascontiguousarray(np.asarray(w_in, dtype=np.float32)[0])
    w_out0 = np.ascontiguousarray(np.asarray(w_out, dtype=np.float32)[0])
    g_in = np.ascontiguousarray(np.asarray(norm_in, dtype=np.float32)[0].reshape(KC, 128).T)
    gq = np.asarray(q_norm, dtype=np.float32)[0]
    gk = np.asarray(k_norm, dtype=np.float32)[0]
    gq_rep = np.ascontiguousarray(np.broadcast_to(gq[None, :], (128, 128)))
    gk_rep = np.ascontiguousarray(np.broadcast_to(gk[None, :], (128, 128)))
    cw = np.asarray(conv_w, dtype=np.float32)[0]
    convw = np.ascontiguousarray(cw.reshape(3, 8, 128).transpose(2, 1, 0))
    convb = np.ascontiguousarray(np.asarray(conv_b, dtype=np.float32)[0].reshape(8, 128).T)
    gfin = np.ascontiguousarray(np.broadcast_to(np.asarray(norm_final, dtype=np.float32)[None, :], (128, D)))
    tabk = _rope_tables()
    cmat = _const_mats()

    if "nc" not in _NC_CACHE:
        _NC_CACHE["nc"] = build_program()
    nc = _NC_CACHE["nc"]

    in_maps = []
    for c in range(NCORES):
        b, h = c // 2, c % 2
        xa = np.ascontiguousarray(x[b])
        xo = np.ascontiguousarray(x[b, h * OWN:(h + 1) * OWN])
        xh = np.zeros((NPASS, 2, D), dtype=np.float32)
        for p in range(NPASS):
            t0 = h * OWN + p * PTOK
            if t0 - 1 >= 0:
                xh[p, 0] = x[b, t0 - 1]
            if t0 + PTOK < S:
                xh[p, 1] = x[b, t0 + PTOK]
        in_maps.append({
            "x_all": xa, "x_own": xo, "x_halo": xh, "w_in": w_in0, "w_out": w_out0, "g_in": g_in,
            "gq_rep": gq_rep, "gk_rep": gk_rep, "convw": convw, "convb": convb, "gfin": gfin,
            "tabk": tabk, "tabq": np.ascontiguousarray(tabk[h * OWN:(h + 1) * OWN]), "cmat": cmat,
        })
    res = run_bass_kernel_spmd(nc, in_maps, core_ids=list(range(NCORES)))
    out = np.empty((NB, S, D), dtype=np.float32)
    for c in range(NCORES):
        b, h = c // 2, c % 2
        out[b, h * OWN:(h + 1) * OWN] = np.asarray(res.results[c]["y"], dtype=np.float32)
    return out
```

```python
import contextlib
import numpy as np
import concourse.bass as bass
import concourse.mybir as mybir
from concourse.bass_utils import run_bass_kernel_spmd

F32 = mybir.dt.float32
BF16 = mybir.dt.bfloat16
AF = mybir.ActivationFunctionType
ALU = mybir.AluOpType
AX = mybir.AxisListType

D = 2048
S = 4096
NB = 4
NCORES = 8
HD = 128
NQH = 8
NKVH = 2
KC = 16
INW = 6656
OWN = 2048
PTOK = 512
NPASS = OWN // PTOK
NBLK = S // PTOK
EPS = 1e-6
OFF_Q, OFF_K, OFF_V, OFF_GA, OFF_CB, OFF_CC, OFF_CX, OFF_GC = 0, 1024, 1280, 1536, 2560, 3584, 4608, 5632
NGROUPS = INW // 128
SQ128 = float(np.sqrt(128.0))


class Buf:
    __slots__ = ("name", "last_w", "readers")

    def __init__(self, name):
        self.name = name
        self.last_w = None
        self.readers = []


class Chan:
    __slots__ = ("name", "count", "sem", "grouped")

    def __init__(self, name, grouped=False):
        self.name = name
        self.count = 0
        self.sem = None
        self.grouped = grouped


class Op:
    __slots__ = ("eng", "fn", "deps", "signal", "sigval", "chan", "dma_val", "waits")


ENGS = ["pe", "act", "dve", "pool", "sp"]


class Prog:
    def __init__(self, nc):
        self.nc = nc
        self.ops = {e: [] for e in ENGS}
        self.all = []
        self.chans = []

    def chan(self, name, grouped=False):
        c = Chan(name, grouped)
        self.chans.append(c)
        return c

    def add(self, eng, fn, r=(), w=(), chan=None):
        op = Op()
        op.eng = eng
        op.fn = fn
        op.signal = False
        op.sigval = 0
        op.chan = chan
        op.dma_val = 0
        deps = set()
        for b in r:
            if b.last_w is not None:
                deps.add(b.last_w)
        for b in w:
            if b.last_w is not None:
                deps.add(b.last_w)
            deps.update(b.readers)
        op.deps = [d for d in deps if not (d.eng == "pe" and eng == "pe" and d.chan is None)]
        for b in r:
            b.readers.append(op)
        for b in w:
            b.readers = []
            b.last_w = op
        if chan is not None:
            chan.count += 1
            op.dma_val = 16 * chan.count
        self.ops[eng].append(op)
        self.all.append(op)
        return op

    def finalize(self):
        for op in self.all:
            for d in op.deps:
                if d.chan is None:
                    d.signal = True
        for e in ENGS:
            c = 0
            for op in self.ops[e]:
                if op.signal:
                    c += 1
                    op.sigval = c
        for e in ENGS:
            waited = {}
            for op in self.ops[e]:
                need = {}
                for d in op.deps:
                    if d.chan is not None:
                        key = ("dma", d.chan)
                        val = 16 * d.chan.count if d.chan.grouped else d.dma_val
                    else:
                        key = ("eng", d.eng)
                        val = d.sigval
                    if need.get(key, 0) < val:
                        need[key] = val
                op.waits = []
                for key, val in need.items():
                    if waited.get(key, 0) >= val:
                        continue
                    waited[key] = val
                    op.waits.append((key, val))

    def emit(self, final_chans):
        nc = self.nc
        self.finalize()
        with contextlib.ExitStack() as es:
            engsem = {e: es.enter_context(nc.semaphore("es_" + e)) for e in ["pe", "act", "dve", "pool"]}
            for c in self.chans:
                c.sem = es.enter_context(nc.semaphore("ch_" + c.name))
            block = es.enter_context(nc.Block())

            def run(e, eng):
                for op in self.ops[e]:
                    for key, val in op.waits:
                        sem = engsem[key[1]] if key[0] == "eng" else key[1].sem
                        eng.wait_ge(sem, val)
                    ins = op.fn(eng)
                    if op.chan is not None:
                        ins.then_inc(op.chan.sem, 16)
                    elif op.signal:
                        ins.then_inc(engsem[e], 1)
                if e == "pool":
                    for c in final_chans:
                        eng.wait_ge(c.sem, 16 * c.count)

            @block.tensor
            def _(eng):
                run("pe", eng)

            @block.scalar
            def _(eng):
                run("act", eng)

            @block.vector
            def _(eng):
                run("dve", eng)

            @block.gpsimd
            def _(eng):
                run("pool", eng)

            @block.sync
            def _(eng):
                run("sp", eng)


def build_program():
    nc = bass.Bass("TRN2", target_bir_lowering=False)
    P = Prog(nc)

    def din(name, shape, dt=F32):
        return nc.dram_tensor(name, list(shape), dt, kind="ExternalInput").ap()

    x_all = din("x_all", [S, D])
    x_own = din("x_own", [OWN, D])
    x_halo = din("x_halo", [NPASS, 2, D])
    w_in = din("w_in", [D, INW])
    w_out = din("w_out", [D, D])
    g_in_d = din("g_in", [128, KC])
    gq_rep_d = din("gq_rep", [128, 128])
    gk_rep_d = din("gk_rep", [128, 128])
    convw_d = din("convw", [128, 8, 3])
    convb_d = din("convb", [128, 8])
    gfin_d = din("gfin", [128, D])
    tabk_d = din("tabk", [S, 2, 64])
    tabq_d = din("tabq", [OWN, 2, 64])
    cmat_d = din("cmat", [128, 2, 128])
    y = nc.dram_tensor("y", [OWN, D], F32, kind="ExternalOutput").ap()
    wscr = nc.dram_tensor("wscr", [NGROUPS, 128, KC * 128], BF16, kind="Internal").ap()

    def sb(name, shape, dt=F32):
        return nc.alloc_sbuf_tensor(name, list(shape), dt)

    NWB = 3
    NPT = 3
    KT = sb("KT", [128, NKVH, S], BF16)
    VV = sb("VV", [128, S // 128, NKVH * HD], BF16)
    WOUT = sb("WOUT", [128, KC, D], BF16)
    HNT = sb("HNT", [128, KC, PTOK + 2], BF16)
    MIXT = sb("MIXT", [128, KC, PTOK], BF16)
    STG = sb("STG", [128, KC, 128], F32)
    WBFALL = sb("WBFALL", [128, NWB * KC * 128], BF16)
    WBF = [WBFALL[:, i * KC * 128:(i + 1) * KC * 128].rearrange("p (a b) -> p a b", a=KC) for i in range(NWB)]
    STG2 = WBFALL[:, 0:2 * KC * 128].bitcast(F32).rearrange("p (a b) -> p a b", a=KC)
    QT = [sb("QT%d" % i, [128, PTOK], BF16) for i in range(2)]
    GATE = [sb("GATE%d" % i, [128, PTOK], F32) for i in range(2)]
    TABS = sb("TABS", [128, 4, 2, 64], F32)
    PTB = [sb("PTB%d" % i, [128, 2, PTOK], BF16) for i in range(NPT)]
    TMP = [sb("TMP%d" % i, [128, PTOK + 2], F32) for i in range(4)]
    QBT = sb("QBT", [128, 4, 128], BF16)
    JNK = sb("JNK", [128, 128], BF16)
    XS = [sb("XS%d" % i, [128, D], F32) for i in range(2)]
    HNBS = [sb("HNB%d" % i, [128, D], BF16) for i in range(2)]
    GFIN = sb("GFIN", [128, D], F32)
    G_IN = sb("G_IN", [128, KC], F32)
    GREP = sb("GREP", [128, 2, 128], F32)
    CONVW = sb("CONVW", [128, 8, 3], F32)
    CONVB = sb("CONVB", [128, 8], F32)
    CMAT32 = sb("CMAT32", [128, 2, 128], F32)
    CMAT = sb("CMAT", [128, 2, 128], BF16)
    SM = sb("SM", [128, 8], F32)
    NEGB = sb("NEGB", [128, 1], F32)
    MHALF = sb("MHALF", [128, 4], F32)
    SSC = sb("SSC", [128, 4], F32)
    RSC = sb("RSC", [128, 4], F32)
    SS4 = sb("SS4", [128, 4], F32)
    RS4 = sb("RS4", [128, 4], F32)
    PS = nc.alloc_psum_tensor("PS", [128, 8, 512], F32)

    IDENT = CMAT[:, 0, :]
    ONES32 = CMAT32[:, 1, :]
    ONES = CMAT[:, 1, :]

    bKT = [Buf("KT%d" % i) for i in range(NBLK)]
    bVV = [Buf("VV%d" % i) for i in range(NBLK)]
    bWOUT = [Buf("WOUT%d" % i) for i in range(KC)]
    bHNT = [Buf("HNT%d" % i) for i in range(5)]
    bMIXT = [Buf("MIXT%d" % i) for i in range(KC)]
    bSTG = Buf("STG")
    bWBF = [Buf("WBF%d" % i) for i in range(NWB)]
    bQT = [Buf("QT%d" % i) for i in range(2)]
    bGATE = [Buf("GATE%d" % i) for i in range(2)]
    bTABS = Buf("TABS")
    bPTB = [Buf("PTB%d" % i) for i in range(NPT)]
    bTMP = [Buf("TMP%d" % i) for i in range(4)]
    bQBT = Buf("QBT")
    bJNK = Buf("JNK")
    bXS = [Buf("XS%d" % i) for i in range(2)]
    bHNBS = [Buf("HNB%d" % i) for i in range(2)]
    bHNA1 = [Buf("HNA1_%d" % i) for i in range(4)]
    bSS = [Buf("SS%d" % i) for i in range(4)]
    bRS = [Buf("RS%d" % i) for i in range(4)]
    bSS4 = Buf("SS4")
    bRS4 = Buf("RS4")
    bPS = [Buf("PS%d" % i) for i in range(8)]
    bCONST = Buf("CONST")
    bSCR = [Buf("SCR%d" % i) for i in range(NGROUPS)]

    cCONST = P.chan("const", grouped=True)
    cSTG = P.chan("stg")
    cSTG2 = P.chan("stg2")
    cWBF = [P.chan("wbf%d" % i) for i in range(NWB)]
    cWBS = [P.chan("wbs%d" % i) for i in range(NWB)]
    cXS = [P.chan("xs%d" % i) for i in range(2)]
    cXST = [P.chan("xst%d" % i) for i in range(2)]
    cTABS = P.chan("tabs")

    const_ops = []

    def cload(dst, src):
        const_ops.append(P.add("sp", lambda e: e.dma_start(out=dst, in_=src), chan=cCONST))

    cload(G_IN[:, :], g_in_d)
    cload(GREP[:, 0, :], gq_rep_d)
    cload(GREP[:, 1, :], gk_rep_d)
    cload(CONVW[:, :, :], convw_d)
    cload(CONVB[:, :], convb_d)
    cload(CMAT32[:, :, :], cmat_d)
    cload(GFIN[:, :], gfin_d)
    bCONST.last_w = const_ops[-1]

    P.add("pool", lambda e: e.memset(MHALF[:, :], -0.5), r=[bCONST], w=[bCONST])
    P.add("dve", lambda e: e.tensor_copy(out=CMAT[:, :, :], in_=CMAT32[:, :, :]), r=[bCONST], w=[bCONST])
    P.add("dve", lambda e: e.reduce_max(out=SM[:, 0:1], in_=GREP[:, 0, :], axis=AX.X, apply_absolute_value=True),
          r=[bCONST], w=[bCONST])
    P.add("dve", lambda e: e.reduce_max(out=SM[:, 1:2], in_=GREP[:, 1, :], axis=AX.X, apply_absolute_value=True),
          r=[bCONST], w=[bCONST])
    P.add("dve", lambda e: e.tensor_tensor(out=SM[:, 2:3], in0=SM[:, 0:1], in1=SM[:, 1:2], op=ALU.mult),
          r=[bCONST], w=[bCONST])
    P.add("dve", lambda e: e.tensor_scalar(out=NEGB[:, :], in0=SM[:, 2:3], scalar1=-SQ128, scalar2=None, op0=ALU.mult),
          r=[bCONST], w=[bCONST])
    P.add("dve", lambda e: e.tensor_scalar(out=GREP[:, 0, :], in0=GREP[:, 0, :], scalar1=1.0 / SQ128, scalar2=None,
                                           op0=ALU.mult), r=[bCONST], w=[bCONST])

    state = {"wb": 0, "xs": 0, "ss": 0, "ps": 0, "qt": 0, "ptb": 0, "gen": 0, "sp": 0, "hnb": 0, "nrot": NWB - 1, "cvspace": 3, "cvspace2": 0, "stgalt": 0}
    scr_ready = [False] * NGROUPS

    def next_ps():
        i = state["ps"]
        state["ps"] = (i + 1) % 8
        return i

    def next_ps_pair():
        i = state["ps"]
        if i % 2:
            i = (i + 1) % 8
        state["ps"] = (i + 2) % 8
        return i

    def next_gen(exclude=()):
        while True:
            i = 4 + state["gen"]
            state["gen"] = (state["gen"] + 1) % 4
            if i not in exclude:
                return i

    def next_spair():
        i = state["sp"]
        state["sp"] = (i + 2) % 4
        return i

    def load_group(col0):
        k = state["wb"] % state["nrot"]
        state["wb"] = (k + 1) % state["nrot"]
        wb, bwb, cwb, cws = WBF[k], bWBF[k], cWBF[k], cWBS[k]
        gidx = col0 // 128
        if scr_ready[gidx]:
            P.add("sp", lambda e: e.dma_start(out=wb[:, :, :].rearrange("p a b -> p (a b)"), in_=wscr[gidx]),
                  r=[bSCR[gidx]], w=[bwb], chan=cwb)
            return wb, bwb
        assert not state.get("strict"), "weight group %d used before its conversion was emitted" % gidx
        srcap = w_in[:, col0:col0 + 128].rearrange("(kc p) c -> p kc c", p=128)
        P.add("sp", lambda e: e.dma_start(out=STG[:, :, :], in_=srcap), w=[bSTG], chan=cSTG)
        gb = G_IN[:, :].unsqueeze(2).to_broadcast([128, KC, 128])
        P.add("dve", lambda e: e.tensor_tensor(out=wb[:, :, :], in0=STG[:, :, :], in1=gb, op=ALU.mult),
              r=[bSTG, bCONST], w=[bwb])
        P.add("pool", lambda e: e.dma_start(out=wscr[gidx], in_=wb[:, :, :].rearrange("p a b -> p (a b)")),
              r=[bwb], w=[bSCR[gidx]], chan=cws)
        scr_ready[gidx] = True
        return wb, bwb

    XENG = "sp"

    def x_issue(src_rows, npart):
        xi = state["xs"]
        state["xs"] = 1 - xi
        xs_ = XS[xi]
        P.add(XENG, lambda e: e.dma_start(out=xs_[0:npart, :], in_=src_rows), w=[bXS[xi]], chan=cXS[xi])
        return xi

    def front_tiles(tiles, alloc_pair, HN=None):
        nxt = x_issue(tiles[0][0], tiles[0][1])
        for i, (src_rows, npart, tok0, hb) in enumerate(tiles):
            xi = nxt
            if i + 1 < len(tiles):
                nxt = x_issue(tiles[i + 1][0], tiles[i + 1][1])
            yield from front_tile(xi, npart, tok0, hb, alloc_pair, HN=HN)

    def front_tile(xi, npart, tok0, hb, alloc_pair, HN=None):
        si = state["ss"]
        state["ss"] = (si + 1) % 4
        xs, bxs = XS[xi], bXS[xi]
        if HN is None:
            HN = HNT
        hi = state["hnb"]
        state["hnb"] = 1 - hi
        HNB, bHNB = HNBS[hi], bHNBS[hi]
        P.add("dve", lambda e: e.memset(SSC[0:npart, si:si + 1], 0.0), w=[bSS[si]])
        P.add("act", lambda e: e.activation(out=HNB[0:npart, :], in_=xs[0:npart, :], func=AF.Square,
                                            accum_out=SSC[0:npart, si:si + 1]),
              r=[bxs], w=[bHNB, bSS[si]])
        yield
        P.add("dve", lambda e: e.tensor_scalar(out=RSC[0:npart, si:si + 1], in0=SSC[0:npart, si:si + 1],
                                               scalar1=1.0 / D, scalar2=EPS, op0=ALU.mult, op1=ALU.add),
              r=[bSS[si]], w=[bRS[si]])
        P.add("pool", lambda e: e.tensor_tensor(out=RSC[0:npart, si:si + 1], in0=RSC[0:npart, si:si + 1],
                                                in1=MHALF[0:npart, 0:1], op=ALU.pow),
              r=[bRS[si], bCONST], w=[bRS[si]])
        yield
        P.add("dve", lambda e: e.tensor_scalar(out=HNB[0:npart, :], in0=xs[0:npart, :],
                                               scalar1=RSC[0:npart, si:si + 1], scalar2=None, op0=ALU.mult),
              r=[bxs, bRS[si]], w=[bHNB])
        yield
        pb = alloc_pair()
        psb = PS[:, pb:pb + 2, :].bitcast(BF16)
        for kc in range(KC):
            o = psb[:, kc // 8, (kc % 8) * 128:(kc % 8) * 128 + npart]

            def tr(e, o=o, kc=kc):
                return e.transpose(out=o, in_=HNB[0:npart, kc * 128:(kc + 1) * 128], identity=IDENT[0:npart, 0:npart])
            P.add("pe", tr, r=[bHNB, bCONST], w=[bPS[pb + kc // 8]])
        yield
        src0 = psb[:, 0, :].rearrange("p (a b) -> p a b", a=8)[:, :, 0:npart]
        src1 = psb[:, 1, :].rearrange("p (a b) -> p a b", a=8)[:, :, 0:npart]
        P.add("act", lambda e: e.copy(out=HN[:, 0:8, tok0:tok0 + npart], in_=src0), r=[bPS[pb]], w=[hb])
        P.add("dve", lambda e: e.tensor_copy(out=HN[:, 8:16, tok0:tok0 + npart], in_=src1), r=[bPS[pb + 1]], w=[hb])
        yield

    def proj_fm(wb, bwb, bank, ntok=PTOK, tok0=0, col0=0, ychunk=4):
        hbs = bHNT[0:4] if tok0 == 0 else [bHNT[4]]
        for kc in range(KC):
            def mm(e, kc=kc):
                return e.matmul(out=PS[:, bank, col0:col0 + ntok], lhsT=wb[:, kc, :], rhs=HNT[:, kc, tok0:tok0 + ntok],
                                start=(kc == 0), stop=(kc == KC - 1))
            P.add("pe", mm, r=[bwb] + hbs, w=[bPS[bank]])
            if kc % ychunk == ychunk - 1:
                yield

    def proj_tm(wb, bwb, bank, ychunk=8, HN=None, hbufs=None):
        if HN is None:
            HN, hbufs = HNT, bHNT
        for tt in range(4):
            for kc in range(KC):
                def mm(e, kc=kc, tt=tt):
                    return e.matmul(out=PS[:, bank, tt * 128:(tt + 1) * 128], lhsT=HN[:, kc, tt * 128:(tt + 1) * 128],
                                    rhs=wb[:, kc, :], start=(kc == 0), stop=(kc == KC - 1))
                P.add("pe", mm, r=[bwb, hbufs[tt]], w=[bPS[bank]])
                if kc % ychunk == ychunk - 1:
                    yield

    def qk_post(bank, gi, out_T, out_bufs, tbank, nspace=0):
        QP = PS[:, bank, :].rearrange("p (t d) -> p t d", t=4)
        P.add("dve", lambda e: e.memset(SS4[:, :], 0.0), w=[bSS4])
        for tt in range(4):
            P.add("act", lambda e, tt=tt: e.activation(out=JNK[:, :], in_=QP[:, tt, :], func=AF.Square,
                                                       accum_out=SS4[:, tt:tt + 1]),
                  r=[bPS[bank]], w=[bJNK, bSS4])
        yield
        P.add("dve", lambda e: e.tensor_scalar(out=RS4[:, :], in0=SS4[:, :], scalar1=1.0 / HD, scalar2=EPS,
                                               op0=ALU.mult, op1=ALU.add), r=[bSS4], w=[bRS4])
        P.add("pool", lambda e: e.tensor_tensor(out=RS4[:, :], in0=RS4[:, :], in1=MHALF[:, :], op=ALU.pow),
              r=[bRS4, bCONST], w=[bRS4])
        yield
        QN = TMP[0][:, 0:PTOK].rearrange("p (t d) -> p t d", t=4)
        rsb = RS4[:, :].unsqueeze(2).to_broadcast([128, 4, 128])
        gbr = GREP[:, gi, :].unsqueeze(1).to_broadcast([128, 4, 128])
        P.add("dve", lambda e: e.tensor_tensor(out=QN, in0=QP, in1=rsb, op=ALU.mult), r=[bPS[bank], bRS4], w=[bTMP[0]])
        P.add("dve", lambda e: e.tensor_tensor(out=QN, in0=QN, in1=gbr, op=ALU.mult), r=[bTMP[0], bCONST], w=[bTMP[0]])
        yield
        X = TMP[0][:, 0:PTOK].rearrange("p (t s h i) -> p t s h i", t=4, s=2, h=2)
        x1, x2 = X[:, :, :, 0, :], X[:, :, :, 1, :]
        Ct = TABS[:, :, 0, :].rearrange("p t (s i) -> p t s i", s=2)
        St = TABS[:, :, 1, :].rearrange("p t (s i) -> p t s i", s=2)
        A = TMP[1][:, 0:256].rearrange("p (t s i) -> p t s i", t=4, s=2)
        B = TMP[2][:, 0:256].rearrange("p (t s i) -> p t s i", t=4, s=2)
        QB = QBT[:, :, :].rearrange("p t (s h i) -> p t s h i", s=2, h=2)
        P.add("dve", lambda e: e.tensor_tensor(out=A, in0=x1, in1=Ct, op=ALU.mult), r=[bTMP[0], bTABS], w=[bTMP[1]])
        P.add("dve", lambda e: e.tensor_tensor(out=B, in0=x2, in1=St, op=ALU.mult), r=[bTMP[0], bTABS], w=[bTMP[2]])
        P.add("dve", lambda e: e.tensor_tensor(out=QB[:, :, :, 0, :], in0=A, in1=B, op=ALU.subtract),
              r=[bTMP[1], bTMP[2]], w=[bQBT])
        yield
        P.add("dve", lambda e: e.tensor_tensor(out=A, in0=x2, in1=Ct, op=ALU.mult), r=[bTMP[0], bTABS], w=[bTMP[1]])
        P.add("dve", lambda e: e.tensor_tensor(out=B, in0=x1, in1=St, op=ALU.mult), r=[bTMP[0], bTABS], w=[bTMP[2]])
        P.add("dve", lambda e: e.tensor_tensor(out=QB[:, :, :, 1, :], in0=A, in1=B, op=ALU.add),
              r=[bTMP[1], bTMP[2]], w=[bQBT])
        yield
        for _ in range(nspace):
            yield
        psb = PS[:, tbank, :].bitcast(BF16)
        for tt in range(4):
            P.add("pe", lambda e, tt=tt: e.transpose(out=psb[:, tt * 128:(tt + 1) * 128], in_=QBT[:, tt, :], identity=IDENT),
                  r=[bQBT, bCONST], w=[bPS[tbank]])
        yield
        P.add("act", lambda e: e.copy(out=out_T, in_=psb[:, 0:PTOK]), r=[bPS[tbank]], w=out_bufs)
        yield

    def load_tabs(src_rows):
        src = src_rows.rearrange("(t p) c f -> p t c f", p=128)
        P.add("sp", lambda e: e.dma_start(out=TABS[:, :, :, :], in_=src), w=[bTABS], chan=cTABS)

    def drain(g):
        for _ in g:
            pass

    def gen_front(ps_i, alloc_pair):
        t0 = ps_i * PTOK
        tiles = [(x_own[t0 + tt * 128:t0 + (tt + 1) * 128, :], 128, tt * 128, bHNT[tt]) for tt in range(4)]
        tiles.append((x_halo[ps_i, :, :], 2, PTOK, bHNT[4]))
        yield from front_tiles(tiles, alloc_pair)

    def load_wout(ec):
        src = w_out[ec * 128:(ec + 1) * 128, :]
        P.add("sp", lambda e: e.dma_start(out=STG[:, :, :].rearrange("p a b -> p (a b)"), in_=src), w=[bSTG], chan=cSTG)
        P.add("dve", lambda e: e.tensor_copy(out=WOUT[:, ec, :], in_=STG[:, :, :].rearrange("p a b -> p (a b)")),
              r=[bSTG], w=[bWOUT[ec]])

    cCVS = P.chan("cvs")

    def gen_convert(col0):
        stg, bstg, cstg = STG, [bSTG], cSTG
        if state["nrot"] < NWB:
            k = NWB - 1
            wb, bwb, cws = WBF[k], bWBF[k], cWBS[k]
            if state["stgalt"] % 2 == 1:
                stg, bstg, cstg = STG2, [bWBF[0], bWBF[1]], cSTG2
            state["stgalt"] += 1
        else:
            wb, bwb, cws = HNBS[0][:, :].rearrange("p (a b) -> p a b", a=KC), bHNBS[0], cCVS
        gidx = col0 // 128
        srcap = w_in[:, col0:col0 + 128].rearrange("(kc p) c -> p kc c", p=128)
        P.add("sp", lambda e: e.dma_start(out=stg[:, :, :], in_=srcap), w=bstg, chan=cstg)
        yield
        for _ in range(state["cvspace"]):
            yield
        gb = G_IN[:, :].unsqueeze(2).to_broadcast([128, KC, 128])
        P.add("dve", lambda e: e.tensor_tensor(out=wb[:, :, :], in0=stg[:, :, :], in1=gb, op=ALU.mult),
              r=bstg + [bCONST], w=[bwb])
        P.add("pool", lambda e: e.dma_start(out=wscr[gidx], in_=wb[:, :, :].rearrange("p a b -> p (a b)")),
              r=[bwb], w=[bSCR[gidx]], chan=cws)
        scr_ready[gidx] = True
        yield
        for _ in range(state["cvspace2"]):
            yield

    def gen_wout(ec):
        src = w_out[ec * 128:(ec + 1) * 128, :]
        P.add("sp", lambda e: e.dma_start(out=STG[:, :, :].rearrange("p a b -> p (a b)"), in_=src), w=[bSTG], chan=cSTG)
        yield
        yield
        yield
        P.add("dve", lambda e: e.tensor_copy(out=WOUT[:, ec, :], in_=STG[:, :, :].rearrange("p a b -> p (a b)")),
              r=[bSTG], w=[bWOUT[ec]])
        yield

    def gen_preconv():
        order = []
        for j in range(8):
            order += [OFF_CC + j * 128, OFF_CX + j * 128, OFF_GC + j * 128, OFF_CB + j * 128]
        for h in range(NQH):
            order += [OFF_Q + h * 128, OFF_GA + h * 128]
        for n, col0 in enumerate(order[:48]):
            yield from gen_convert(col0)
            if n >= 40 and n % 2 == 1:
                for ec in range(n - 41, n - 39):
                    yield from gen_wout(ec)
        for ec in range(8, KC):
            yield from gen_wout(ec)

    preconv = gen_preconv()

    def interleave(main, fill, ratio):
        n = 0
        for _ in main:
            n += 1
            if n % ratio == 0 and fill is not None:
                next(fill, None)

    HNA = [(HNT, bHNT), (MIXT, bHNA1)]

    rrs = {"kv": 0, "fr": 0}

    def kv_bank():
        i = rrs["kv"]
        rrs["kv"] = (i + 1) % 4
        return i

    def fr_pair():
        i = rrs["fr"]
        rrs["fr"] = 1 - i
        return 4 + 2 * i

    def gen_front_A(tb):
        HN, hb = HNA[tb % 2]
        tiles = [(x_all[tb * PTOK + tt * 128:tb * PTOK + (tt + 1) * 128, :], 128, tt * 128, hb[tt]) for tt in range(4)]
        yield from front_tiles(tiles, fr_pair, HN=HN)

    kv_cols = [OFF_K, OFF_K + 128, OFF_V, OFF_V + 128]
    WKV = [WOUT[:, :, g * 128:(g + 1) * 128] for g in range(4)]
    bWKV = [Buf("WKV%d" % g) for g in range(4)]
    kvq = {"n": 0}

    def kv_weights():
        g = kvq["n"] % 4
        kvq["n"] += 1
        return WKV[g], bWKV[g]

    def gen_kv_convert(g):
        srcap = w_in[:, kv_cols[g]:kv_cols[g] + 128].rearrange("(kc p) c -> p kc c", p=128)
        P.add("sp", lambda e: e.dma_start(out=STG[:, :, :], in_=srcap), w=[bSTG], chan=cSTG)
        yield
        yield
        gb = G_IN[:, :].unsqueeze(2).to_broadcast([128, KC, 128])
        P.add("dve", lambda e: e.tensor_tensor(out=WKV[g], in0=STG[:, :, :], in1=gb, op=ALU.mult),
              r=[bSTG, bCONST], w=[bWKV[g]])
        yield

    def gen_k(tb):
        HN, hb = HNA[tb % 2]
        load_tabs(tabk_d[tb * PTOK:(tb + 1) * PTOK])
        for kvh in range(NKVH):
            wb, bwb = WKV[kvh], bWKV[kvh]
            bank = kvh
            yield from proj_tm(wb, bwb, bank, HN=HN, hbufs=hb)
            yield from qk_post(bank, 1, KT[:, kvh, tb * PTOK:(tb + 1) * PTOK], [bKT[tb]], 2, nspace=3)

    def gen_v(tb):
        HN, hb = HNA[tb % 2]
        for kvh in range(NKVH):
            wb, bwb = WKV[2 + kvh], bWKV[2 + kvh]
            bank = 3
            yield from proj_tm(wb, bwb, bank, HN=HN, hbufs=hb)

            def evac(e, bank=bank, kvh=kvh, tb=tb):
                return e.copy(out=VV[:, tb * 4:(tb + 1) * 4, kvh * HD:(kvh + 1) * HD],
                              in_=PS[:, bank, :].rearrange("p (a b) -> p a b", a=4))
            P.add("act", evac, r=[bPS[bank]], w=[bVV[tb]])
            yield

    def chain(*gens):
        for g in gens:
            if g is not None:
                yield from g

    def take(g, n):
        for _ in range(n):
            if next(g, "END") == "END":
                return
            yield

    def rr(streams):
        live = [[g, w] for g, w in streams if g is not None]
        while live:
            for item in list(live):
                g, w = item
                for _ in range(w):
                    if next(g, "END") == "END":
                        live.remove(item)
                        break

    rr([(gen_front_A(0), 1),
        (chain(*[gen_kv_convert(g) for g in range(4)]), 1)])
    for tb in range(NBLK):
        nxt = gen_front_A(tb + 1) if tb + 1 < NBLK else gen_front(0, fr_pair)
        rr([(gen_k(tb), 2), (gen_v(tb), 1), (nxt, 1), (take(preconv, 20), 1)])
    alias_ops = []
    for b in bHNA1:
        alias_ops += list(b.readers)
        if b.last_w is not None:
            alias_ops.append(b.last_w)
    for b in bMIXT:
        b.readers = list(alias_ops)
    alias_ops = []
    for b in bWKV:
        alias_ops += list(b.readers)
        if b.last_w is not None:
            alias_ops.append(b.last_w)
    for b in bWOUT:
        b.readers = list(alias_ops)

    wh = {}

    def issue(h, which):
        if h < NQH:
            wh[(h, which)] = load_group((OFF_Q if which == "q" else OFF_GA) + h * 128)

    def gen_head_inproj(ps_i, h, qi, alloc):
        wb, bwb = wh.pop((h, "q"))
        bq = alloc()
        yield from proj_tm(wb, bwb, bq)
        yield from qk_post(bq, 0, QT[qi][:, :], [bQT[qi]], alloc(), nspace=8)
        wb, bwb = wh.pop((h, "ga"))
        bga = alloc()
        yield from proj_fm(wb, bwb, bga)
        G = GATE[qi]
        P.add("act", lambda e: e.activation(out=G[:, :], in_=PS[:, bga, :], func=AF.Tanh, scale=0.5),
              r=[bPS[bga]], w=[bGATE[qi]])
        P.add("dve", lambda e: e.scalar_tensor_tensor(out=G[:, :], in0=G[:, :], scalar=1.0, in1=PS[:, bga, :],
                                                      op0=ALU.add, op1=ALU.mult), r=[bPS[bga], bGATE[qi]], w=[bGATE[qi]])
        yield

    SUMENG = "pool"

    def qk_emit(kvh, qi, k2):
        sp_ = next_spair()
        for i in range(2):
            kt = 2 * k2 + i

            def mm(e, kt=kt, i=i, sp_=sp_):
                return e.matmul(out=PS[:, sp_ + i, :], lhsT=KT[:, kvh, kt * 128:(kt + 1) * 128], rhs=QT[qi][:, :],
                                start=True, stop=True)
            P.add("pe", mm, r=[bKT[kt // 4], bQT[qi]], w=[bPS[sp_ + i]])
        return sp_

    def attention(h, qi, ob, lb, filler, nfill, bg=None, sp0=None, nxt=None, deferred=None):
        kvh = h // (NQH // NKVH)
        nkt = S // 128
        nst = nkt // 2
        sps = {0: sp0 if sp0 is not None else qk_emit(kvh, qi, 0)}
        sp_next = None
        for k2 in range(nst):
            if k2 + 1 < nst:
                sps[k2 + 1] = qk_emit(kvh, qi, k2 + 1)
            elif nxt is not None:
                sp_next = qk_emit(nxt[1], nxt[0], 0)
            sp_ = sps[k2]
            pi = state["ptb"]
            state["ptb"] = (pi + 1) % NPT

            def ex(e, sp_=sp_, pi=pi):
                return e.activation(out=PTB[pi][:, :, :], in_=PS[:, sp_:sp_ + 2, :], func=AF.Exp,
                                    bias=NEGB[:, 0:1], scale=1.0)
            P.add("act", ex, r=[bPS[sp_], bPS[sp_ + 1], bCONST], w=[bPTB[pi]])
            for i in range(2):
                kt = 2 * k2 + i

                def pv(e, kt=kt, i=i, pi=pi):
                    return e.matmul(out=PS[:, ob, :], lhsT=VV[:, kt, kvh * HD:(kvh + 1) * HD], rhs=PTB[pi][:, i, :],
                                    start=(kt == 0), stop=(kt == nkt - 1))
                P.add("pe", pv, r=[bVV[kt // 4], bPTB[pi]], w=[bPS[ob]])
            for i in range(2):
                kt = 2 * k2 + i

                def ls(e, kt=kt, i=i, pi=pi):
                    return e.matmul(out=PS[:, lb, :], lhsT=ONES, rhs=PTB[pi][:, i, :],
                                    start=(kt == 0), stop=(kt == nkt - 1))
                P.add("pe", ls, r=[bCONST, bPTB[pi]], w=[bPS[lb]])
            if k2 == 0 and deferred is not None:
                deferred()
            if filler is not None:
                for _ in range(nfill):
                    next(filler, None)
                if k2 == nst - 3 and nxt is not None:
                    drain(filler)
            if k2 == 3:
                issue(h + 2, "q")
            if k2 == 10:
                issue(h + 2, "ga")
            if bg is not None:
                next(bg, None)
        def fin():
            RL = TMP[3][:, 0:PTOK]
            O = RL
            P.add("act", lambda e: e.copy(out=RL, in_=PS[:, lb, :]), r=[bPS[lb]], w=[bTMP[3]])
            P.add("dve", lambda e: e.reciprocal(out=RL, in_=RL), r=[bTMP[3]], w=[bTMP[3]])
            P.add("dve", lambda e: e.tensor_tensor(out=O, in0=PS[:, ob, :], in1=RL, op=ALU.mult),
                  r=[bPS[ob], bTMP[3]], w=[bTMP[3]])
            P.add("dve", lambda e: e.scalar_tensor_tensor(out=MIXT[:, h, :], in0=O, scalar=0.5, in1=GATE[qi][:, :],
                                                          op0=ALU.mult, op1=ALU.mult),
                  r=[bTMP[3], bGATE[qi]], w=[bMIXT[h]])
        return fin, sp_next

    def conv_chunk(j):
        hbank = next_ps()
        CCE, U, Y, SG = TMP[0], TMP[1], TMP[2], TMP[3]
        wb, bwb = load_group(OFF_CC + j * 128)
        bcc = next_ps()
        drain(proj_fm(wb, bwb, bcc))
        drain(proj_fm(wb, bwb, hbank, ntok=2, tok0=PTOK, col0=0))
        P.add("act", lambda e: e.copy(out=CCE[:, 1:PTOK + 1], in_=PS[:, bcc, :]), r=[bPS[bcc]], w=[bTMP[0]])
        P.add("act", lambda e: e.copy(out=CCE[:, 0:1], in_=PS[:, hbank, 0:1]), r=[bPS[hbank]], w=[bTMP[0]])
        P.add("act", lambda e: e.copy(out=CCE[:, PTOK + 1:PTOK + 2], in_=PS[:, hbank, 1:2]), r=[bPS[hbank]], w=[bTMP[0]])
        wb, bwb = load_group(OFF_CX + j * 128)
        bcx = next_ps()
        drain(proj_fm(wb, bwb, bcx))
        drain(proj_fm(wb, bwb, hbank, ntok=2, tok0=PTOK, col0=8))
        P.add("dve", lambda e: e.tensor_tensor(out=U[:, 1:PTOK + 1], in0=PS[:, bcx, :], in1=CCE[:, 1:PTOK + 1], op=ALU.mult),
              r=[bPS[bcx], bTMP[0]], w=[bTMP[1]])
        P.add("dve", lambda e: e.tensor_tensor(out=U[:, 0:1], in0=PS[:, hbank, 8:9], in1=CCE[:, 0:1], op=ALU.mult),
              r=[bPS[hbank], bTMP[0]], w=[bTMP[1]])
        P.add("dve", lambda e: e.tensor_tensor(out=U[:, PTOK + 1:PTOK + 2], in0=PS[:, hbank, 9:10],
                                               in1=CCE[:, PTOK + 1:PTOK + 2], op=ALU.mult),
              r=[bPS[hbank], bTMP[0]], w=[bTMP[1]])
        P.add("dve", lambda e: e.tensor_scalar(out=Y[:, 0:PTOK], in0=U[:, 0:PTOK], scalar1=CONVW[:, j, 0:1],
                                               scalar2=CONVB[:, j:j + 1], op0=ALU.mult, op1=ALU.add),
              r=[bTMP[1], bCONST], w=[bTMP[2]])
        P.add("dve", lambda e: e.scalar_tensor_tensor(out=Y[:, 0:PTOK], in0=U[:, 1:PTOK + 1], scalar=CONVW[:, j, 1:2],
                                                      in1=Y[:, 0:PTOK], op0=ALU.mult, op1=ALU.add),
              r=[bTMP[1], bTMP[2], bCONST], w=[bTMP[2]])
        P.add("dve", lambda e: e.scalar_tensor_tensor(out=Y[:, 0:PTOK], in0=U[:, 2:PTOK + 2], scalar=CONVW[:, j, 2:3],
                                                      in1=Y[:, 0:PTOK], op0=ALU.mult, op1=ALU.add),
              r=[bTMP[1], bTMP[2], bCONST], w=[bTMP[2]])
        wb, bwb = load_group(OFF_GC + j * 128)
        bgc = next_ps()
        drain(proj_fm(wb, bwb, bgc))
        P.add("act", lambda e: e.activation(out=SG[:, 0:PTOK], in_=PS[:, bgc, :], func=AF.Tanh, scale=0.5),
              r=[bPS[bgc]], w=[bTMP[3]])
        P.add("dve", lambda e: e.scalar_tensor_tensor(out=SG[:, 0:PTOK], in0=SG[:, 0:PTOK], scalar=1.0, in1=PS[:, bgc, :],
                                                      op0=ALU.add, op1=ALU.mult), r=[bPS[bgc], bTMP[3]], w=[bTMP[3]])
        wb, bwb = load_group(OFF_CB + j * 128)
        bcb = next_ps()
        drain(proj_fm(wb, bwb, bcb))
        P.add("dve", lambda e: e.tensor_tensor(out=Y[:, 0:PTOK], in0=PS[:, bcb, :], in1=Y[:, 0:PTOK], op=ALU.mult),
              r=[bPS[bcb], bTMP[2]], w=[bTMP[2]])
        P.add("dve", lambda e: e.scalar_tensor_tensor(out=MIXT[:, 8 + j, :], in0=Y[:, 0:PTOK], scalar=0.5,
                                                      in1=SG[:, 0:PTOK], op0=ALU.mult, op1=ALU.mult),
              r=[bTMP[2], bTMP[3]], w=[bMIXT[8 + j]])

    def out_proj(ps_i):
        t0 = ps_i * PTOK
        for tt in range(4):
            r0 = t0 + tt * 128
            xi = state["xs"]
            state["xs"] = 1 - xi
            si = state["ss"]
            state["ss"] = (si + 1) % 4
            xs, bxs = XS[xi], bXS[xi]
            hi = state["hnb"]
            state["hnb"] = 1 - hi
            HNB, bHNB = HNBS[hi], bHNBS[hi]
            P.add(XENG, lambda e, xs=xs, r0=r0: e.dma_start(out=xs[:, :], in_=x_own[r0:r0 + 128, :]), w=[bxs], chan=cXS[xi])
            pb = 0 if state["ps"] < 4 else 4
            state["ps"] = (pb + 4) % 8
            for dmb in range(4):
                for ec in range(KC):
                    def mm(e, ec=ec, dmb=dmb, pb=pb, tt=tt):
                        return e.matmul(out=PS[:, pb + dmb, :], lhsT=MIXT[:, ec, tt * 128:(tt + 1) * 128],
                                        rhs=WOUT[:, ec, dmb * 512:(dmb + 1) * 512], start=(ec == 0), stop=(ec == KC - 1))
                    P.add("pe", mm, r=[bMIXT[ec], bWOUT[ec]], w=[bPS[pb + dmb]])
            pbufs = [bPS[pb + i] for i in range(4)]
            P.add("dve", lambda e, xs=xs, pb=pb: e.tensor_tensor(out=xs[:, :], in0=PS[:, pb:pb + 4, :].rearrange("p a b -> p (a b)"),
                                                                 in1=xs[:, :], op=ALU.add), r=pbufs + [bxs], w=[bxs])
            P.add("dve", lambda e, si=si: e.memset(SSC[:, si:si + 1], 0.0), w=[bSS[si]])
            P.add("act", lambda e, xs=xs, si=si, HNB=HNB: e.activation(out=HNB[:, :], in_=xs[:, :], func=AF.Square,
                                                                       accum_out=SSC[:, si:si + 1]), r=[bxs], w=[bHNB, bSS[si]])
            P.add("dve", lambda e, si=si: e.tensor_scalar(out=RSC[:, si:si + 1], in0=SSC[:, si:si + 1], scalar1=1.0 / D,
                                                          scalar2=EPS, op0=ALU.mult, op1=ALU.add), r=[bSS[si]], w=[bRS[si]])
            P.add("pool", lambda e, si=si: e.tensor_tensor(out=RSC[:, si:si + 1], in0=RSC[:, si:si + 1],
                                                           in1=MHALF[:, 0:1], op=ALU.pow), r=[bRS[si], bCONST], w=[bRS[si]])
            P.add("dve", lambda e, xs=xs, si=si: e.scalar_tensor_tensor(out=xs[:, :], in0=xs[:, :], scalar=RSC[:, si:si + 1],
                                                                        in1=GFIN[:, :], op0=ALU.mult, op1=ALU.mult),
                  r=[bxs, bRS[si], bCONST], w=[bxs])
            P.add("pool", lambda e, xs=xs, r0=r0: e.dma_start(out=y[r0:r0 + 128, :], in_=xs[:, :]), r=[bxs], chan=cXST[xi])

    state["nrot"] = NWB
    state["strict"] = True
    state["cvspace"] = 2
    state["cvspace2"] = 0
    for ps_i in range(NPASS):
        t0 = ps_i * PTOK
        load_tabs(tabq_d[t0:t0 + PTOK])
        for j in range(8):
            conv_chunk(j)
            if ps_i == 0:
                drain(take(preconv, 4))
        state["gen"] = 0
        state["sp"] = 0
        qis = [h % 2 for h in range(NQH)]
        issue(0, "q")
        issue(0, "ga")
        issue(1, "q")
        drain(gen_head_inproj(ps_i, 0, qis[0], lambda: next_gen()))
        issue(1, "ga")
        deferred, sp0 = None, None
        for h in range(NQH):
            ob = 4 + 2 * (h % 2)
            lb = ob + 1
            nxt = (qis[h + 1], (h + 1) // (NQH // NKVH)) if h + 1 < NQH else None
            if h + 1 < NQH:
                pob = 4 + 2 * ((h + 1) % 2)
                seq = iter([pob + 1, pob, pob + 1, pob, pob + 1, pob])
                filler = gen_head_inproj(ps_i, h + 1, qis[h + 1], lambda seq=seq: next(seq))
                nfill = 2
            elif ps_i + 1 < NPASS:
                filler = gen_front(ps_i + 1, lambda ob=ob: 4 if ob == 6 else 6)
                nfill = 2
            else:
                filler, nfill = None, 0
            deferred, sp0 = attention(h, qis[h], ob, lb, filler, nfill, bg=(preconv if ps_i == 0 else None),
                                      sp0=sp0, nxt=nxt, deferred=deferred)
            if filler is not None:
                drain(filler)
        deferred()
        if ps_i == 0:
            drain(preconv)
        out_proj(ps_i)

    P.emit(final_chans=cXST + cWBS)
    return nc


def _rope_tables():
    grid_w = 64
    rows = S // grid_w
    row = np.repeat(np.arange(rows, dtype=np.float32), grid_w)
    col = np.tile(np.arange(grid_w, dtype=np.float32), rows)
    inv_freq = (np.float32(10000.0) ** (-(np.arange(0, 64, 2, dtype=np.float32)) / np.float32(64.0))).astype(np.float32)
    ang_r = (row[:, None] * inv_freq[None, :]).astype(np.float32)
    ang_c = (col[:, None] * inv_freq[None, :]).astype(np.float32)
    tab = np.zeros((S, 2, 64), dtype=np.float32)
    tab[:, 0, 0:32] = np.cos(ang_r)
    tab[:, 0, 32:64] = np.cos(ang_c)
    tab[:, 1, 0:32] = np.sin(ang_r)
    tab[:, 1, 32:64] = np.sin(ang_c)
    return tab


def _const_mats():
    cm = np.zeros((128, 2, 128), dtype=np.float32)
    cm[:, 0, :] = np.eye(128, dtype=np.float32)
    cm[:, 1, :] = 1.0
    return cm


_NC_CACHE = {}


def kernel(x, norm_in, w_in, q_norm, k_norm, conv_w, conv_b, w_out, norm_final):
    x = np.asarray(x, dtype=np.float32)
    w_in0 = np.ascontiguousarray(np.asarray(w_in, dtype=np.float32)[0])
    w_out0 = np.ascontiguousarray(np.asarray(w_out, dtype=np.float32)[0])
    g_in = np.ascontiguousarray(np.asarray(norm_in, dtype=np.float32)[0].reshape(KC, 128).T)
    gq = np.asarray(q_norm, dtype=np.float32)[0]
    gk = np.asarray(k_norm, dtype=np.float32)[0]
    gq_rep = np.ascontiguousarray(np.broadcast_to(gq[None, :], (128, 128)))
    gk_rep = np.ascontiguousarray(np.broadcast_to(gk[None, :], (128, 128)))
    cw = np.asarray(conv_w, dtype=np.float32)[0]
    convw = np.ascontiguousarray(cw.reshape(3, 8, 128).transpose(2, 1, 0))
    convb = np.ascontiguousarray(np.asarray(conv_b, dtype=np.float32)[0].reshape(8, 128).T)
    gfin = np.ascontiguousarray(np.broadcast_to(np.asarray(norm_final, dtype=np.float32)[None, :], (128, D)))
    tabk = _rope_tables()
    cmat = _const_mats()

    if "nc" not in _NC_CACHE:
        _NC_CACHE["nc"] = build_program()
    nc = _NC_CACHE["nc"]

    in_maps = []
    for c in range(NCORES):
        b, h = c // 2, c % 2
        xa = np.ascontiguousarray(x[b])
        xo = np.ascontiguousarray(x[b, h * OWN:(h + 1) * OWN])
        xh = np.zeros((NPASS, 2, D), dtype=np.float32)
        for p in range(NPASS):
            t0 = h * OWN + p * PTOK
            if t0 - 1 >= 0:
                xh[p, 0] = x[b, t0 - 1]
            if t0 + PTOK < S:
                xh[p, 1] = x[b, t0 + PTOK]
        in_maps.append({
            "x_all": xa, "x_own": xo, "x_halo": xh, "w_in": w_in0, "w_out": w_out0, "g_in": g_in,
            "gq_rep": gq_rep, "gk_rep": gk_rep, "convw": convw, "convb": convb, "gfin": gfin,
            "tabk": tabk, "tabq": np.ascontiguousarray(tabk[h * OWN:(h + 1) * OWN]), "cmat": cmat,
        })
    res = run_bass_kernel_spmd(nc, in_maps, core_ids=list(range(NCORES)))
    out = np.empty((NB, S, D), dtype=np.float32)
    for c in range(NCORES):
        b, h = c // 2, c % 2
        out[b, h * OWN:(h + 1) * OWN] = np.asarray(res.results[c]["y"], dtype=np.float32)
    return out
```
